# Optimizing a Trainium2 kernel written in Bass

```python
import jax, jax.numpy as jnp
from jax import lax
import numpy as np

D_MODEL = 1024
BATCH = 4
SEQ = 8192
DEPTH = 2

CHUNK = 64
SB_HEADS = 8
SB_HEAD_DIM = D_MODEL // 16
SB_WIDTH = SB_HEADS * SB_HEAD_DIM
SB_BLOCK = 128
HG_HEAD_DIM = D_MODEL // 8
HG_HEADS = 4
HG_WIDTH = HG_HEADS * HG_HEAD_DIM
HG_CHUNK = CHUNK // 4
MIX_WIDTH = SB_WIDTH + HG_WIDTH
IN_WIDTH = 3 * SB_WIDTH + 4 * HG_WIDTH
SPLIT_POINTS = (SB_WIDTH, 2 * SB_WIDTH, 3 * SB_WIDTH,
                3 * SB_WIDTH + HG_WIDTH, 3 * SB_WIDTH + 2 * HG_WIDTH,
                3 * SB_WIDTH + 3 * HG_WIDTH)
CONV_WIDTH = 31
D_FF = 4 * D_MODEL
N_AB = (DEPTH + 1) // 2
N_CONV = DEPTH // 2
RMS_EPS = 1e-6
LN_EPS = 1e-5

kernel_name = "hybrid_stickbreak_hgrn2_conformer_trunk"


def rmsnorm(x, g):
    xf = x.astype(jnp.float32)
    y = xf * lax.rsqrt(jnp.mean(xf * xf, axis=-1, keepdims=True) + RMS_EPS)
    return (y * g).astype(x.dtype)


def stick_breaking_attention(q, k, v):
    b, s, h, dh = q.shape
    nb = s // SB_BLOCK
    scale = dh ** -0.5
    kh = k.transpose(0, 2, 1, 3)
    vh = v.transpose(0, 2, 1, 3)
    q_blocks = jnp.moveaxis(q.transpose(0, 2, 1, 3).reshape(b, h, nb, SB_BLOCK, dh), 2, 0)
    key_pos = jnp.arange(s)

    def one_block(args):
        qb, blk = args
        z = jnp.einsum('bhqd,bhkd->bhqk', qb, kh).astype(jnp.float32) * scale
        q_pos = blk * SB_BLOCK + jnp.arange(SB_BLOCK)
        mask = key_pos[None, :] < q_pos[:, None]
        log_beta = jax.nn.log_sigmoid(z)
        log_keep = jnp.where(mask, log_beta - z, 0.0)
        log_keep_after = lax.cumsum(log_keep, axis=3, reverse=True) - log_keep
        w = jnp.where(mask, jnp.exp(log_beta + log_keep_after), 0.0)
        return jnp.einsum('bhqk,bhkd->bhqd', w.astype(vh.dtype), vh)

    out = lax.map(one_block, (q_blocks, jnp.arange(nb)))
    out = jnp.moveaxis(out, 0, 2).reshape(b, h, s, dh).transpose(0, 2, 1, 3)
    return out.reshape(b, s, h * dh)


def hgrn2(q, f_raw, i, gate, lb, norm_g):
    b, s, h, d = q.shape
    n = s // HG_CHUNK
    f32 = jnp.float32
    f = lb + (1.0 - lb) * jax.nn.sigmoid(f_raw.astype(f32))
    g = jnp.log(f)
    kk = 1.0 - f

    def chunked(t):
        return t.transpose(0, 2, 1, 3).reshape(b, h, n, HG_CHUNK, d)

    qc, gc, kc, vc = chunked(q.astype(f32)), chunked(g), chunked(kk), chunked(i.astype(f32))
    G = jnp.cumsum(gc, axis=3)
    G_last = G[:, :, :, -1:, :]
    q_dec = qc * jnp.exp(G)
    k_intra = kc * jnp.exp(-G)
    k_state = kc * jnp.exp(G_last - G)
    decay = jnp.exp(G_last[:, :, :, 0, :])
    causal = jnp.tril(jnp.ones((HG_CHUNK, HG_CHUNK), dtype=bool))
    scores = jnp.where(causal, jnp.einsum('bhncd,bhnsd->bhncs', q_dec, k_intra), 0.0)
    o_intra = jnp.einsum('bhncs,bhnsv->bhncv', scores, vc)

    def step(state, xs):
        qd, ks, v, dec = xs
        o = jnp.einsum('bhcd,bhdv->bhcv', qd, state)
        state = state * dec[..., None] + jnp.einsum('bhcd,bhcv->bhdv', ks, v)
        return state, o

    xs = (jnp.moveaxis(q_dec, 2, 0), jnp.moveaxis(k_state, 2, 0),
          jnp.moveaxis(vc, 2, 0), jnp.moveaxis(decay, 2, 0))
    _, o_inter = lax.scan(step, jnp.zeros((b, h, d, d), f32), xs)
    o = o_intra + jnp.moveaxis(o_inter, 0, 2)
    o = o.reshape(b, h, s, d).transpose(0, 2, 1, 3)
    o = o * lax.rsqrt(jnp.mean(o * o, axis=-1, keepdims=True) + RMS_EPS) * norm_g
    o = o * jax.nn.silu(gate.astype(f32))
    return o.reshape(b, s, h * d).astype(q.dtype)


def parallel_ab_mixer(u, w_in, w_out, lb, hg_norm_g):
    b, s, _ = u.shape
    proj = u @ w_in
    sb_q, sb_k, sb_v, hg_q, hg_f, hg_i, hg_g = jnp.split(proj, SPLIT_POINTS, axis=-1)
    sb_shape = (b, s, SB_HEADS, SB_HEAD_DIM)
    hg_shape = (b, s, HG_HEADS, HG_HEAD_DIM)
    o_sb = stick_breaking_attention(sb_q.reshape(sb_shape), sb_k.reshape(sb_shape), sb_v.reshape(sb_shape))
    o_hg = hgrn2(hg_q.reshape(hg_shape), hg_f.reshape(hg_shape), hg_i.reshape(hg_shape),
                 hg_g.reshape(hg_shape), lb.reshape(HG_HEADS, HG_HEAD_DIM), hg_norm_g)
    return jnp.concatenate([o_sb, o_hg], axis=-1) @ w_out


def conformer_conv(u, w_glu, b_glu, w_dw, b_dw, ln_g, ln_b, w_pw, b_pw):
    a = jax.nn.glu(u @ w_glu + b_glu, axis=-1)
    y = lax.conv_general_dilated(a, w_dw[:, None, :], window_strides=(1,),
                                 padding=[(CONV_WIDTH - 1, 0)],
                                 dimension_numbers=('NWC', 'WIO', 'NWC'),
                                 feature_group_count=D_MODEL) + b_dw
    yf = y.astype(jnp.float32)
    mu = jnp.mean(yf, axis=-1, keepdims=True)
    var = jnp.mean(jnp.square(yf - mu), axis=-1, keepdims=True)
    y = jax.nn.silu((yf - mu) * lax.rsqrt(var + LN_EPS) * ln_g + ln_b).astype(u.dtype)
    return y @ w_pw + b_pw


def squared_relu_mlp(u, w1, w2):
    return jnp.square(jax.nn.relu(u @ w1)) @ w2


def setup_inputs(seed: int = 0) -> dict:
    key = jax.random.key(seed)
    ks = jax.random.split(key, 20)
    nrm = jax.random.normal
    f32 = jnp.float32
    return {
        "x": nrm(ks[0], (BATCH, SEQ, D_MODEL), f32),
        "norm_mix_g": 1.0 + 0.02 * nrm(ks[1], (DEPTH, D_MODEL), f32),
        "norm_ffn_g": 1.0 + 0.02 * nrm(ks[2], (DEPTH, D_MODEL), f32),
        "w_in_ab": nrm(ks[3], (N_AB, D_MODEL, IN_WIDTH), f32) * D_MODEL ** -0.5,
        "w_out_ab": nrm(ks[4], (N_AB, MIX_WIDTH, D_MODEL), f32) * MIX_WIDTH ** -0.5,
        "hg_lb_logits": 0.1 * nrm(ks[5], (DEPTH + 1, HG_WIDTH), f32),
        "hg_norm_g": 1.0 + 0.02 * nrm(ks[6], (N_AB, HG_HEADS, HG_HEAD_DIM), f32),
        "conv_w_glu": nrm(ks[7], (N_CONV, D_MODEL, 2 * D_MODEL), f32) * D_MODEL ** -0.5,
        "conv_b_glu": 0.02 * nrm(ks[8], (N_CONV, 2 * D_MODEL), f32),
        "conv_w_dw": nrm(ks[9], (N_CONV, CONV_WIDTH, D_MODEL), f32) * CONV_WIDTH ** -0.5,
        "conv_b_dw": 0.02 * nrm(ks[10], (N_CONV, D_MODEL), f32),
        "conv_ln_g": 1.0 + 0.02 * nrm(ks[11], (N_CONV, D_MODEL), f32),
        "conv_ln_b": 0.02 * nrm(ks[12], (N_CONV, D_MODEL), f32),
        "conv_w_pw": nrm(ks[13], (N_CONV, D_MODEL, D_MODEL), f32) * D_MODEL ** -0.5,
        "conv_b_pw": 0.02 * nrm(ks[14], (N_CONV, D_MODEL), f32),
        "w_ff1": nrm(ks[15], (DEPTH, D_MODEL, D_FF), f32) * D_MODEL ** -0.5,
        "w_ff2": nrm(ks[16], (DEPTH, D_FF, D_MODEL), f32) * D_FF ** -0.5,
        "final_norm_g": 1.0 + 0.02 * nrm(ks[17], (D_MODEL,), f32),
    }


def reference(x, norm_mix_g, norm_ffn_g, w_in_ab, w_out_ab, hg_lb_logits, hg_norm_g,
              conv_w_glu, conv_b_glu, conv_w_dw, conv_b_dw, conv_ln_g, conv_ln_b,
              conv_w_pw, conv_b_pw, w_ff1, w_ff2, final_norm_g):
    lower_bounds = jnp.cumsum(jax.nn.softmax(hg_lb_logits.astype(jnp.float32), axis=0), axis=0)
    h = x
    for layer in range(DEPTH):
        j = layer // 2
        u = rmsnorm(h, norm_mix_g[layer])
        if layer % 2 == 0:
            h = h + parallel_ab_mixer(u, w_in_ab[j], w_out_ab[j], lower_bounds[layer], hg_norm_g[j])
        else:
            h = h + conformer_conv(u, conv_w_glu[j], conv_b_glu[j], conv_w_dw[j], conv_b_dw[j],
                                   conv_ln_g[j], conv_ln_b[j], conv_w_pw[j], conv_b_pw[j])
        u = rmsnorm(h, norm_ffn_g[layer])
        h = h + squared_relu_mlp(u, w_ff1[layer], w_ff2[layer])
    return rmsnorm(h, final_norm_g)
```

```python
import contextlib
import numpy as np
import ml_dtypes
import concourse.bass as bass
import concourse.mybir as mybir
from concourse.bass_utils import run_bass_kernel_spmd

F32 = mybir.dt.float32
BF16 = mybir.dt.bfloat16
F32R = mybir.dt.float32r
AF = mybir.ActivationFunctionType
ALU = mybir.AluOpType

D = 1024
DFF = 4096
SEQ = 8192
BATCH = 4
NCORES = 8
RMS_EPS = 1e-6
LN_EPS = 1e-5
CONVW = 31


class Res:
    __slots__ = ("name", "last_w", "readers", "excl")

    def __init__(self, name, excl=False):
        self.name = name
        self.last_w = None
        self.readers = []
        self.excl = excl


class Op:
    __slots__ = ("eng", "fn", "deps", "sig", "sigcount", "dma", "dsem", "dval", "prewait", "inc")

    def __init__(self, eng, fn, dma):
        self.eng = eng
        self.fn = fn
        self.deps = []
        self.sig = False
        self.sigcount = 0
        self.dma = dma
        self.dsem = None
        self.dval = 0
        self.prewait = None
        self.inc = 16


class Prog:
    ENGS = ("pe", "act", "dve", "pool", "sp")
    DMA_ENGS = ("sp", "pool", "act")

    def __init__(self, nc, ndma_sems=8):
        self.nc = nc
        self.ops = {e: [] for e in self.ENGS}
        self.ndma = {e: 0 for e in self.ENGS}
        self.K = ndma_sems
        self.all_ops = []

    def _add(self, eng, fn, reads, writes, dma):
        op = Op(eng, fn, dma)
        deps = []
        for r in reads:
            if r.last_w is not None:
                deps.append(r.last_w)
            if r.excl:
                deps.extend(x for x in r.readers if x.eng != eng)
        for w in writes:
            if w.last_w is not None:
                deps.append(w.last_w)
            deps.extend(w.readers)
        for r in reads:
            r.readers.append(op)
        for w in writes:
            w.last_w = op
            w.readers = []
        seen = set()
        for d in deps:
            if id(d) not in seen and d is not op:
                seen.add(id(d))
                op.deps.append(d)
        if dma:
            i = self.ndma[eng]
            self.ndma[eng] += 1
            op.dsem = (eng, i % self.K)
            op.dval = 16 * (i // self.K + 1)
            if i >= self.K:
                op.prewait = (op.dsem, 16 * (i // self.K))
        self.ops[eng].append(op)
        self.all_ops.append(op)
        return op

    def op(self, eng, fn, reads=(), writes=()):
        return self._add(eng, fn, reads, writes, False)

    def dma(self, eng, out, in_, reads=(), writes=()):
        return self._add(eng, lambda e: e.dma_start(out=out, in_=in_), reads, writes, True)

    def mm(self, out, lhsT, rhs, start, stop, reads, writes):
        return self.op("pe", lambda e: e.matmul(out, lhsT=lhsT, rhs=rhs, start=start, stop=stop),
                       reads, writes)

    def tr(self, out, in_, ident, reads, writes):
        return self.op("pe", lambda e: e.transpose(out=out, in_=in_, identity=ident), reads, writes)

    def act(self, out, in_, func, reads, writes, **kw):
        return self.op("act", lambda e: e.activation(out=out, in_=in_, func=func, **kw), reads, writes)

    def ts(self, eng, out, in0, s1, s2, op0, op1, reads, writes):
        if op1 is None:
            return self.op(eng, lambda e: e.tensor_scalar(out=out, in0=in0, scalar1=s1, scalar2=None, op0=op0),
                           reads, writes)
        return self.op(eng, lambda e: e.tensor_scalar(out=out, in0=in0, scalar1=s1, scalar2=s2, op0=op0, op1=op1),
                       reads, writes)

    def tt(self, eng, out, in0, in1, op, reads, writes):
        return self.op(eng, lambda e: e.tensor_tensor(out=out, in0=in0, in1=in1, op=op), reads, writes)

    def stt(self, out, in0, scalar, in1, op0, op1, reads, writes):
        return self.op("dve", lambda e: e.scalar_tensor_tensor(out=out, in0=in0, scalar=scalar, in1=in1,
                                                               op0=op0, op1=op1), reads, writes)

    def copy(self, eng, out, in_, reads, writes):
        return self.op(eng, lambda e: e.tensor_copy(out=out, in_=in_), reads, writes)

    def memset(self, eng, out, val, writes):
        return self.op(eng, lambda e: e.memset(out, val), (), writes)

    def cc(self, fn, reads, writes):
        op = self._add("pool", fn, reads, writes, False)
        op.dma = True
        self.ncc = getattr(self, "ncc", 0) + 1
        op.dsem = ("cc", 0)
        op.dval = self.ncc
        op.inc = 1
        return op

    def barrier(self):
        tails = []
        for e in self.ENGS:
            comp = [o for o in self.ops[e] if not o.dma]
            if comp:
                tails.append(comp[-1])
            dm = [o for o in self.ops[e] if o.dma and o.dsem[0] != "cc"]
            tails.extend(dm[-self.K:])
        for e in self.ENGS:
            op = self.op(e, lambda eng: eng.nop(), (), ())
            op.deps.extend(t for t in tails if t is not op)

    def _skip(self, d, op):
        return (not d.dma) and d.eng == op.eng and op.eng == "pe" and not op.dma

    def alloc_sems(self, es):
        nc = self.nc
        self.sems = {"eng": {e: es.enter_context(nc.semaphore(f"se_{e}")) for e in self.ENGS},
                     "dma": {(e, i): es.enter_context(nc.semaphore(f"sd_{e}{i}"))
                             for e in self.DMA_ENGS for i in range(self.K)}}
        self.sems["dma"][("cc", 0)] = es.enter_context(nc.semaphore("s_cc"))
        self.emitted = {e: 0 for e in self.ENGS}
        self.sigc = {e: 0 for e in self.ENGS}
        self.waited = {e: {} for e in self.ENGS}
        self.nwaits = 0

    def plan(self):
        seg = {e: self.ops[e][self.emitted[e]:] for e in self.ENGS}
        segset = set(id(o) for e in self.ENGS for o in seg[e])
        for e in self.ENGS:
            for op in seg[e]:
                for d in op.deps:
                    if d.dma or self._skip(d, op):
                        continue
                    if id(d) not in segset:
                        assert d.sig, "cross-segment dependency on a non-signalling op"
                    d.sig = True
        for e in self.ENGS:
            for op in seg[e]:
                if op.sig and not op.dma:
                    self.sigc[e] += 1
                    op.sigcount = self.sigc[e]
        plans = {}
        for e in self.ENGS:
            waited = self.waited[e]
            plan = []
            for op in seg[e]:
                ws = {}
                if op.prewait is not None:
                    k, v = op.prewait
                    ws[("d",) + k] = v
                for d in op.deps:
                    if d.dma:
                        k = ("d",) + d.dsem
                        v = d.dval
                    else:
                        if self._skip(d, op):
                            continue
                        k = ("e", d.eng)
                        v = d.sigcount
                    if ws.get(k, 0) < v:
                        ws[k] = v
                wl = []
                for k, v in ws.items():
                    if waited.get(k, 0) >= v:
                        continue
                    waited[k] = v
                    wl.append((k, v))
                self.nwaits += len(wl)
                plan.append((op, wl))
            plans[e] = plan
            self.emitted[e] = len(self.ops[e])
        return plans

    def run_engine(self, e, engine, plans):
        esem = self.sems["eng"]
        dsem = self.sems["dma"]
        for op, wl in plans[e]:
            for k, v in wl:
                if k[0] == "d":
                    engine.wait_ge(dsem[(k[1], k[2])], v)
                else:
                    engine.wait_ge(esem[k[1]], v)
            ins = op.fn(engine)
            if op.dma:
                ins.then_inc(dsem[op.dsem], getattr(op, "inc", 16))
            elif op.sig:
                ins.then_inc(esem[e], 1)

    def emit_segment(self):
        plans = self.plan()
        P = self
        with self.nc.Block() as block:
            @block.sync
            def _(e):
                P.run_engine("sp", e, plans)

            @block.tensor
            def _(e):
                P.run_engine("pe", e, plans)

            @block.scalar
            def _(e):
                P.run_engine("act", e, plans)

            @block.vector
            def _(e):
                P.run_engine("dve", e, plans)

            @block.gpsimd
            def _(e):
                P.run_engine("pool", e, plans)

    def finish(self, es):
        self.alloc_sems(es)
        self.emit_segment()


class Ctx:
    def __init__(self, nc, es, prefix=""):
        self.nc = nc
        self.es = es
        self.sb_bytes = 0
        self.prefix = prefix

    def sb(self, name, shape, dt):
        n = 1
        for s in shape[1:]:
            n *= s
        self.sb_bytes += n * (4 if dt in (F32, F32R) else 2)
        return self.es.enter_context(self.nc.sbuf_tensor("s_" + self.prefix + name, shape, dt))

    def ps(self, name):
        return self.es.enter_context(self.nc.psum_tensor("p_" + self.prefix + name, [128, 512], F32))


NTB = 4096 + 128
NWT_B = 40
CP_G_FFN0 = 0
CP_G_MIX1 = 8
CP_G_FFN1 = 16
CP_BA = 24
CP_BB = 32
CP_BDW = 40
CP_LNG = 48
CP_LNB = 56
CP_HALO = 64
CP_F0 = 65
CP_F1 = 66
CP_WDW = 72
NCOLP = CP_WDW + 8 * CONVW


def precast_ops(io):
    w32, w16 = io["w32"], io["w16"]
    out = []
    for i in range(w32.shape[0]):
        for hh in range(2):
            out.append((w16[i, :, hh * 2048:(hh + 1) * 2048], w32[i, :, hh * 2048:(hh + 1) * 2048], f"w16_{i}_{hh}"))
    return out


def build_phase_b(nc, P, C, io, tiles, precast_done=None, gathered=None, gathered_res=None):
    xB, w32, w16, colp_d, rowp_d, ident_d, outd = (io[k] for k in
                                                  ("xB", "w32", "w16", "colp", "rowp", "ident", "out"))
    oTd = io.get("oT")
    NW = 5
    TM = 512
    h = C.sb("h", [128, 4, D], F32)
    utm = C.sb("utm", [128, 4, D], BF16)
    uT = C.sb("uT", [128, 8, TM], BF16)
    hidT = C.sb("hidT", [128, 32, TM], BF16)
    wb = [C.sb(f"wb{i}", [128, 4096], BF16) for i in range(NW)]
    oTt = C.sb("oTt", [128, 8, TM], BF16)
    if gathered is not None:
        oTa = C.sb("oTa", [128, 8, TM], BF16)
        oTb = C.sb("oTb", [128, 8, TM], BF16)
    aT = C.sb("aT", [128, 8, 32 + TM], F32)
    ybuf = C.sb("ybuf", [128, 8, TM], F32)
    zT = C.sb("zT", [128, 8, TM], BF16)
    sig = [C.sb(f"sig{i}", [128, TM], F32) for i in range(2)]
    rtmp = [C.sb(f"rtmp{i}", [128, TM], BF16) for i in range(2)]
    junk = C.sb("junk", [128, D], BF16)
    ysq = C.sb("ysq", [128, TM], F32)
    lnm = C.sb("lnm", [128, TM], F32)
    lnm2 = C.sb("lnm2", [128, TM], F32)
    lnr = C.sb("lnr", [128, TM], F32)
    lnt = [C.sb(f"lnt{i}", [128, TM], F32) for i in range(2)]
    ssq = C.sb("ssq", [128, 4], F32)
    rstd = C.sb("rstd", [128, 4], F32)
    colp = C.sb("colp", [128, NCOLP], F32)
    rowp = C.sb("rowp", [128, 2, D], F32)
    ident = C.sb("ident", [128, 128], BF16)
    ones_r = C.sb("ones_r", [128, 128], F32)
    pb = [C.ps(f"pb{i}") for i in range(8)]

    R = {}

    def res(n):
        if n not in R:
            R[n] = Res(n)
        return R[n]

    r_h = [res(f"h{s}") for s in range(4)]
    r_utm = [res(f"utm{s}") for s in range(4)]
    r_uT = [res(f"uT{c}") for c in range(8)]
    r_hid = [res(f"hid{c}") for c in range(32)]
    r_wb = [res(f"wb{i}") for i in range(NW)]
    r_pb = [R.setdefault(f"pb{i}", Res(f"pb{i}", True)) for i in range(8)]
    r_aT = [res(f"aT{c}") for c in range(8)]
    r_y = [res(f"y{c}") for c in range(8)]
    r_zT = [res(f"zT{c}") for c in range(8)]
    r_sig = [res("sig0"), res("sig1")]
    r_rt = [res("rt0"), res("rt1")]
    r_lnt = [res("lnt0"), res("lnt1")]
    r_const = res("const")
    r_w16 = [res(f"w16_{i}") for i in range(NWT_B)]

    P.dma("sp", colp[:], colp_d[:, :], writes=[r_const])
    P.dma("sp", rowp[:], rowp_d[:, :, :], writes=[r_const])
    P.dma("pool", ident[:], ident_d[:, :], writes=[r_const])
    P.memset("pool", ones_r[:], 1.0 / D, writes=[r_const])
    for c in range(8):
        P.memset("pool", aT[:, c, 0:32], 0.0, writes=[r_aT[c]])

    if precast_done is None:
        for op_args in precast_ops(io):
            P.dma("pool", op_args[0], op_args[1], writes=[res(op_args[2])])
    else:
        for n, r in precast_done.items():
            R[n] = r

    wstate = {"n": 0}

    def wload(tile_idx):
        i = wstate["n"] % NW
        wstate["n"] += 1
        P.dma("sp", wb[i][:], w16[tile_idx, :, :], reads=[res(f"w16_{tile_idx}_0"), res(f"w16_{tile_idx}_1")],
              writes=[r_wb[i]])
        return wb[i], r_wb[i]

    def rmsnorm_to_uT(nsub, gcol):
        T = nsub * 128
        for s in range(nsub):
            P.act(junk[:], h[:, s, :], AF.Square, [r_h[s]], [res("junk"), res("ssq")], accum_out=ssq[:, s:s + 1])
        P.act(rstd[:, 0:nsub], ssq[:, 0:nsub], AF.Ln, [res("ssq")], [res("rstd")], scale=1.0 / D, bias=RMS_EPS)
        P.act(rstd[:, 0:nsub], rstd[:, 0:nsub], AF.Exp, [res("rstd")], [res("rstd")], scale=-0.5)
        for s in range(nsub):
            P.ts("dve", utm[:, s, :], h[:, s, :], rstd[:, s:s + 1], None, ALU.mult, None,
                 [r_h[s], res("rstd")], [r_utm[s]])
        for c in range(8):
            bank = 6 + (c % 2)
            pT = pb[bank][:].bitcast(BF16)
            for s in range(nsub):
                P.tr(pT[:, s * 128:(s + 1) * 128], utm[:, s, c * 128:(c + 1) * 128], ident[:],
                     [r_utm[s], r_const], [r_pb[bank]])
            if c % 2 == 0:
                P.ts("dve", uT[:, c, 0:T], pT[:, 0:T], colp[:, gcol + c:gcol + c + 1], None, ALU.mult, None,
                     [r_pb[bank], r_const], [r_uT[c]])
            else:
                P.act(uT[:, c, 0:T], pT[:, 0:T], AF.Copy, [r_pb[bank], r_const], [r_uT[c]],
                      scale=colp[:, gcol + c:gcol + c + 1])

    def proj_tokmajor(nsub, srcT, r_src, wbase):
        for half in range(2):
            wt, rw = wload(wbase + half)
            wv = wt[:].rearrange("p (k n) -> p k n", k=8)
            for s in range(nsub):
                for k in range(8):
                    P.mm(pb[s][:, :], srcT[:, k, s * 128:(s + 1) * 128], wv[:, k, :], k == 0, k == 7,
                         [r_src[k], rw], [r_pb[s]])
                P.tt("dve", h[:, s, half * 512:(half + 1) * 512], pb[s][:, :], h[:, s, half * 512:(half + 1) * 512],
                     ALU.add, [r_pb[s], r_h[s]], [r_h[s]])

    def ffn(nsub, w1base, w2base):
        T = nsub * 128
        for j in range(8):
            wt, rw = wload(w1base + j)
            wv = wt[:].rearrange("p (k n) -> p k n", k=8)
            for cc in range(4):
                c = 4 * j + cc
                bank = 4 + (c % 2)
                for k in range(8):
                    P.mm(pb[bank][:, 0:T], wv[:, k, cc * 128:(cc + 1) * 128], uT[:, k, 0:T], k == 0, k == 7,
                         [rw, r_uT[k]], [r_pb[bank]])
                P.act(rtmp[c % 2][:, 0:T], pb[bank][:, 0:T], AF.Relu, [r_pb[bank]], [r_rt[c % 2]])
                P.tt("pool", hidT[:, c, 0:T], rtmp[c % 2][:, 0:T], rtmp[c % 2][:, 0:T], ALU.mult,
                     [r_rt[c % 2]], [r_hid[c]])
        for half in range(2):
            for cg in range(4):
                wt, rw = wload(w2base + half * 4 + cg)
                wv = wt[:].rearrange("p (k n) -> p k n", k=8)
                for s in range(nsub):
                    for cc in range(8):
                        c = cg * 8 + cc
                        P.mm(pb[s][:, :], hidT[:, c, s * 128:(s + 1) * 128], wv[:, cc, :], c == 0, c == 31,
                             [r_hid[c], rw], [r_pb[s]])
            for s in range(nsub):
                P.tt("dve", h[:, s, half * 512:(half + 1) * 512], pb[s][:, :], h[:, s, half * 512:(half + 1) * 512],
                     ALU.add, [r_pb[s], r_h[s]], [r_h[s]])

    prev_T = None
    for ti, (tok0, nsub, full) in enumerate(tiles):
        T = nsub * 128
        for s in range(nsub):
            P.dma("sp", h[:, s, :], xB[tok0 + s * 128: tok0 + (s + 1) * 128, :], writes=[r_h[s]])
        r_oTt = [res(f"oTt{k}") for k in range(8)]
        if gathered is None:
            P.dma("sp", oTt[:, :, 0:T], oTd[:, :, tok0:tok0 + T], writes=r_oTt)
        else:
            P.dma("sp", oTa[:, :, 0:T], gathered(0, tok0, T), reads=[gathered_res], writes=[res("oTa")])
            P.dma("sp", oTb[:, :, 0:T], gathered(1, tok0, T), reads=[gathered_res], writes=[res("oTb")])
            P.ts("pool", oTb[:, :, 0:T], oTb[:, :, 0:T], colp[:, CP_F1:CP_F1 + 1], None, ALU.mult, None,
                 [res("oTb"), r_const], [res("oTb")])
            P.stt(oTt[:, :, 0:T], oTa[:, :, 0:T], colp[:, CP_F0:CP_F0 + 1], oTb[:, :, 0:T], ALU.mult, ALU.add,
                  [res("oTa"), res("oTb"), r_const], r_oTt)
        proj_tokmajor(nsub, oTt, [res(f"oTt{k}") for k in range(8)], 0)
        rmsnorm_to_uT(nsub, CP_G_FFN0)
        ffn(nsub, 2, 10)
        rmsnorm_to_uT(nsub, CP_G_MIX1)
        if prev_T is not None:
            for c in range(8):
                if ti == 1:
                    P.ts("pool", aT[:, c, 2:32], aT[:, c, 2 + prev_T:32 + prev_T], colp[:, CP_HALO:CP_HALO + 1], None,
                         ALU.mult, None, [r_aT[c], r_const], [r_aT[c]])
                else:
                    P.copy("pool", aT[:, c, 2:32], aT[:, c, 2 + prev_T:32 + prev_T], [r_aT[c]], [r_aT[c]])
        for j in range(4):
            wt, rw = wload(18 + j)
            wv = wt[:].rearrange("p (k n) -> p k n", k=8)
            for q in range(2):
                c = 2 * j + q
                ba_, bb_ = (4, 5) if c % 2 == 0 else (6, 7)
                for k in range(8):
                    P.mm(pb[ba_][:, 0:T], wv[:, k, q * 128:(q + 1) * 128], uT[:, k, 0:T], k == 0, k == 7,
                         [rw, r_uT[k]], [r_pb[ba_]])
                for k in range(8):
                    P.mm(pb[bb_][:, 0:T], wv[:, k, (2 + q) * 128:(3 + q) * 128], uT[:, k, 0:T], k == 0, k == 7,
                         [rw, r_uT[k]], [r_pb[bb_]])
                P.act(sig[c % 2][:, 0:T], pb[bb_][:, 0:T], AF.Sigmoid, [r_pb[bb_], r_const], [r_sig[c % 2]],
                      bias=colp[:, CP_BB + c:CP_BB + c + 1])
                P.stt(aT[:, c, 32:32 + T], pb[ba_][:, 0:T], colp[:, CP_BA + c:CP_BA + c + 1], sig[c % 2][:, 0:T],
                      ALU.add, ALU.mult, [r_pb[ba_], r_sig[c % 2], r_const], [r_aT[c]])
        prev_T = T
        if not full:
            continue
        for k in range(CONVW):
            for c in range(8):
                wc = CP_WDW + c * CONVW
                if c < 6:
                    if k == 0:
                        P.ts("dve", ybuf[:, c, 0:T], aT[:, c, 2:2 + T], colp[:, wc:wc + 1],
                             colp[:, CP_BDW + c:CP_BDW + c + 1], ALU.mult, ALU.add, [r_aT[c], r_const], [r_y[c]])
                    else:
                        P.stt(ybuf[:, c, 0:T], aT[:, c, 2 + k:2 + k + T], colp[:, wc + k:wc + k + 1],
                              ybuf[:, c, 0:T], ALU.mult, ALU.add, [r_aT[c], r_const, r_y[c]], [r_y[c]])
                else:
                    if k == 0:
                        P.ts("pool", ybuf[:, c, 0:T], aT[:, c, 2:2 + T], colp[:, wc:wc + 1],
                             colp[:, CP_BDW + c:CP_BDW + c + 1], ALU.mult, ALU.add, [r_aT[c], r_const], [r_y[c]])
                    else:
                        tmp = lnt[c % 2]
                        P.ts("pool", tmp[:, 0:T], aT[:, c, 2 + k:2 + k + T], colp[:, wc + k:wc + k + 1], None,
                             ALU.mult, None, [r_aT[c], r_const], [r_lnt[c % 2]])
                        P.tt("pool", ybuf[:, c, 0:T], ybuf[:, c, 0:T], tmp[:, 0:T], ALU.add,
                             [r_y[c], r_lnt[c % 2]], [r_y[c]])
        for c in range(8):
            P.mm(pb[6][:, 0:T], ones_r[:], ybuf[:, c, 0:T], c == 0, c == 7,
                 [r_const, r_y[c]], [r_pb[6]])
        for c in range(8):
            P.act(ysq[:, 0:T], ybuf[:, c, 0:T], AF.Square, [r_y[c]], [res("ysq")])
            P.mm(pb[7][:, 0:T], ones_r[:], ysq[:, 0:T], c == 0, c == 7,
                 [r_const, res("ysq")], [r_pb[7]])
        P.act(lnm[:, 0:T], pb[6][:, 0:T], AF.Identity, [r_pb[6]], [res("lnm")])
        P.act(lnm2[:, 0:T], pb[6][:, 0:T], AF.Square, [r_pb[6]], [res("lnm2")])
        P.tt("dve", lnr[:, 0:T], pb[7][:, 0:T], lnm2[:, 0:T], ALU.subtract, [r_pb[7], res("lnm2")], [res("lnr")])
        P.act(lnr[:, 0:T], lnr[:, 0:T], AF.Ln, [res("lnr")], [res("lnr")], bias=LN_EPS)
        P.act(lnr[:, 0:T], lnr[:, 0:T], AF.Exp, [res("lnr")], [res("lnr")], scale=-0.5)
        for c in range(8):
            i2 = c % 2
            P.tt("pool", lnt[i2][:, 0:T], ybuf[:, c, 0:T], lnm[:, 0:T], ALU.subtract, [r_y[c], res("lnm")],
                 [r_lnt[i2]])
            P.tt("dve", lnt[i2][:, 0:T], lnt[i2][:, 0:T], lnr[:, 0:T], ALU.mult, [r_lnt[i2], res("lnr")],
                 [r_lnt[i2]])
            P.act(zT[:, c, 0:T], lnt[i2][:, 0:T], AF.Silu, [r_lnt[i2], r_const], [r_zT[c]],
                  scale=colp[:, CP_LNG + c:CP_LNG + c + 1], bias=colp[:, CP_LNB + c:CP_LNB + c + 1])
        for s in range(nsub):
            P.tt("pool", h[:, s, :], h[:, s, :], rowp[:, 0, :], ALU.add, [r_h[s], r_const], [r_h[s]])
        proj_tokmajor(nsub, zT, r_zT, 22)
        rmsnorm_to_uT(nsub, CP_G_FFN1)
        ffn(nsub, 24, 32)
        for s in range(nsub):
            P.act(junk[:], h[:, s, :], AF.Square, [r_h[s]], [res("junk"), res("ssq")], accum_out=ssq[:, s:s + 1])
        P.act(rstd[:, 0:nsub], ssq[:, 0:nsub], AF.Ln, [res("ssq")], [res("rstd")], scale=1.0 / D, bias=RMS_EPS)
        P.act(rstd[:, 0:nsub], rstd[:, 0:nsub], AF.Exp, [res("rstd")], [res("rstd")], scale=-0.5)
        ov = ybuf[:].rearrange("p c t -> p (c t)").rearrange("p (s d) -> p s d", s=4)
        outs = []
        for s in range(nsub):
            P.stt(ov[:, s, :], h[:, s, :], rstd[:, s:s + 1], rowp[:, 1, :], ALU.mult, ALU.mult,
                  [r_h[s], res("rstd"), r_const], r_y)
            outs.append(P.dma("pool", outd[tok0 - 128 + s * 128: tok0 - 128 + (s + 1) * 128, :], ov[:, s, :],
                              reads=r_y))
        io.setdefault("_outs", []).extend(outs)


def make_phase_b_nc(tiles=None):
    nc = bass.Bass("TRN2", target_bir_lowering=False)
    if tiles is None:
        tiles = [(0, 1, False)] + [(128 + 512 * i, 4, True) for i in range(8)]
    io = {
        "xB": nc.dram_tensor("xB", [NTB, D], F32, kind="ExternalInput").ap(),
        "oT": nc.dram_tensor("oT", [128, 8, NTB], BF16, kind="ExternalInput").ap(),
        "w32": nc.dram_tensor("w32", [NWT_B, 128, 4096], F32, kind="ExternalInput").ap(),
        "w16": nc.dram_tensor("w16", [NWT_B, 128, 4096], BF16, kind="Internal").ap(),
        "colp": nc.dram_tensor("colp", [128, NCOLP], F32, kind="ExternalInput").ap(),
        "rowp": nc.dram_tensor("rowp", [128, 2, D], F32, kind="ExternalInput").ap(),
        "ident": nc.dram_tensor("ident", [128, 128], F32, kind="ExternalInput").ap(),
        "out": nc.dram_tensor("out", [NTB - 128, D], F32, kind="ExternalOutput").ap(),
    }
    P = Prog(nc)
    with contextlib.ExitStack() as es:
        C = Ctx(nc, es)
        build_phase_b(nc, P, C, io, tiles)
        fin = P.op("sp", lambda e: e.nop(), (), ())
        fin.deps.extend(io["_outs"])
        P.finish(es)
    return nc, P


WA_COLS = 1792
CA_G = 0
CA_LOG = 8
CA_NG = 14
NCOLA = 16
KA_IDENT, KA_NEGT, KA_NEGU, KA_MASKD, KA_MASKH, KA_ONES, KA_ZERO = range(7)
NCA = 7


def build_phase_a(nc, P, C, io, ntiles, do_attn=True, do_hg=True, out_fn=None, tile_hook=None, out_res=None):
    xA, wA32, colp_d, cst_d, mres_d = (io[k] for k in ("xA", "wA32", "colpA", "constA", "mres"))
    if out_fn is None:
        oTA = io["oTA"]
        out_fn = lambda q, j: oTA[:, q, j * 512:(j + 1) * 512]
    out_w = [] if out_res is None else [out_res]
    TM = 512
    NT = SEQ // TM
    wA = C.sb("wA", [128, 8, WA_COLS], BF16)
    KT = C.sb("KT", [128, 2, SEQ], BF16)
    V = C.sb("V", [128, SEQ // 128, 256], BF16)
    xt = C.sb("xt", [128, 4, D], F32)
    utm = C.sb("utm", [128, 4, D], BF16)
    uT = C.sb("uT", [128, 8, TM], BF16)
    QT = C.sb("QT", [128, 2, TM], BF16)
    hq = C.sb("hq", [128, 2, TM], F32)
    hsgs = C.sb("hsgs", [128, 2, TM], F32)
    hgt = C.sb("hgt", [128, 2, TM], F32)
    hi = C.sb("hi", [128, 4, 256], BF16)
    junk = C.sb("junk", [128, D], BF16)
    ssq = C.sb("ssq", [128, 4], F32)
    rstd = C.sb("rstd", [128, 4], F32)
    colp = C.sb("colpA", [128, NCOLA], F32)
    cst = C.sb("cst", [128, NCA, 128], BF16)
    mres = C.sb("mres", [128, TM], F32)
    lbt = C.sb("lbt", [128, 2, 4], F32)
    lb = C.sb("lb", [128, 2], F32)
    oml = C.sb("oml", [128, 2], F32)
    eZ = [C.sb(f"eZ{i}", [128, 2, TM], F32) for i in range(3)]
    spb = [C.sb(f"spb{i}", [128, 2, TM], BF16) for i in range(2)]
    Xb = [C.sb(f"Xb{i}", [128, 2, TM], BF16) for i in range(2)]
    Wb = [C.sb(f"Wb{i}", [128, 2, TM], BF16) for i in range(2)]
    osb = [C.sb(f"osb{i}", [128, TM], BF16) for i in range(2)]
    hff = C.sb("hff", [128, TM], F32)
    hg_ = C.sb("hg_", [128, TM], F32)
    hkk = C.sb("hkk", [128, TM], F32)
    hG = C.sb("hG", [128, TM], F32)
    heG = C.sb("heG", [128, TM], F32)
    hks = C.sb("hks", [128, TM], BF16)
    hqd = C.sb("hqd", [128, TM], BF16)
    hkib = C.sb("hkib", [128, TM], BF16)
    kstm = C.sb("kstm", [128, 4, 128], BF16)
    smk = [C.sb(f"smk{i}", [128, 128], BF16) for i in range(2)]
    Sst = [[C.sb(f"S{hd}_{i}", [128, 128], F32) for i in range(2)] for hd in range(2)]
    Sbf = [[C.sb(f"Sb{hd}_{i}", [128, 128], BF16) for i in range(2)] for hd in range(2)]
    osq = C.sb("osq", [128, TM], BF16)
    hrs = C.sb("hrs", [128, TM], F32)
    gsig = C.sb("gsig", [128, 2, TM], F32)
    ho1 = C.sb("ho1", [128, TM], F32)
    ohT = [C.sb(f"ohT{i}", [128, TM], BF16) for i in range(2)]
    pz0 = C.es.enter_context(nc.psum_tensor("p_z0", [128, 2, 512], F32))
    pz1 = C.es.enter_context(nc.psum_tensor("p_z1", [128, 2, 512], F32))
    pc = C.es.enter_context(nc.psum_tensor("p_c", [128, 2, 512], F32))
    psing = [C.ps(f"pa{i}") for i in range(6, 8)]
    pzz = (pz0, pz1)

    def pbk(b):
        if b < 2:
            return pz0[:, b, :]
        if b < 4:
            return pz1[:, b - 2, :]
        if b < 6:
            return pc[:, b - 4, :]
        return psing[b - 6][:]

    R = {}

    def res(n):
        if n not in R:
            R[n] = Res(n)
        return R[n]

    r_pb = [R.setdefault(f"pb{i}", Res(f"pb{i}", True)) for i in range(8)]
    r_const = res("const")
    r_xt = [res(f"xt{s}") for s in range(4)]
    r_utm = [res(f"utm{s}") for s in range(4)]
    r_uT = [res(f"uT{c}") for c in range(8)]
    r_KT = [[res(f"KT{p}_{j}") for j in range(NT)] for p in range(2)]
    r_V = [res(f"V{j}") for j in range(NT)]

    ident = cst[:, KA_IDENT, :]
    negT = cst[:, KA_NEGT, :]
    negU = cst[:, KA_NEGU, :]
    maskD = cst[:, KA_MASKD, :]
    maskH = cst[:, KA_MASKH, :]
    ones128 = cst[:, KA_ONES, :]
    zeroM = cst[:, KA_ZERO, :]

    P.dma("sp", colp[:], colp_d[:, :], writes=[r_const])
    P.dma("sp", mres[:], mres_d[:, :], writes=[r_const])
    P.dma("pool", cst[:], cst_d[:, :, :], writes=[r_const])
    for k in range(8):
        P.dma("pool", wA[:, k, :], wA32[:, k, :], writes=[res("wA")])
    lg = colp[:, CA_LOG:CA_LOG + 6].rearrange("p (h r) -> p h r", h=2)
    P.act(lbt[:, :, 0:3], lg, AF.Exp, [r_const], [res("lbt")])
    for hd in range(2):
        P.tt("dve", lbt[:, hd, 3:4], lbt[:, hd, 0:1], lbt[:, hd, 1:2], ALU.add, [res("lbt")], [res("lbt")])
        P.tt("dve", lbt[:, hd, 3:4], lbt[:, hd, 3:4], lbt[:, hd, 2:3], ALU.add, [res("lbt")], [res("lbt")])
        P.op("dve", (lambda hd: lambda e: e.reciprocal(out=lbt[:, hd, 3:4], in_=lbt[:, hd, 3:4]))(hd),
             [res("lbt")], [res("lbt")])
        P.tt("dve", lb[:, hd:hd + 1], lbt[:, hd, 0:1], lbt[:, hd, 3:4], ALU.mult, [res("lbt")], [res("lb")])
    P.ts("dve", oml[:], lb[:], -1.0, 1.0, ALU.mult, ALU.add, [res("lb")], [res("lb2")])
    for hd in range(2):
        P.memset("pool", Sst[hd][0][:], 0.0, [res(f"S{hd}_0")])
        P.memset("pool", Sbf[hd][0][:], 0.0, [res(f"Sb{hd}_0")])
    hstep = [0, 0]

    outs = io.setdefault("_outs", [])
    io["_zeroM"] = zeroM
    for j in range(ntiles):
        if tile_hook is not None:
            tile_hook(j)
        for s in range(4):
            P.dma("sp", xt[:, s, :], xA[j * TM + s * 128: j * TM + (s + 1) * 128, :], writes=[r_xt[s]])
        for s in range(4):
            P.act(junk[:], xt[:, s, :], AF.Square, [r_xt[s]], [res("junk"), res("ssq")], accum_out=ssq[:, s:s + 1])
        P.act(rstd[:], ssq[:], AF.Ln, [res("ssq")], [res("rstd")], scale=1.0 / D, bias=RMS_EPS)
        P.act(rstd[:], rstd[:], AF.Exp, [res("rstd")], [res("rstd")], scale=-0.5)
        for s in range(4):
            P.ts("dve", utm[:, s, :], xt[:, s, :], rstd[:, s:s + 1], None, ALU.mult, None,
                 [r_xt[s], res("rstd")], [r_utm[s]])
        for c in range(8):
            bank = 6 + (c % 2)
            pT = pbk(bank).bitcast(BF16)
            for s in range(4):
                P.tr(pT[:, s * 128:(s + 1) * 128], utm[:, s, c * 128:(c + 1) * 128], ident,
                     [r_utm[s], r_const], [r_pb[bank]])
            if c % 2 == 0:
                P.ts("dve", uT[:, c, :], pT[:, 0:TM], colp[:, CA_G + c:CA_G + c + 1], None, ALU.mult, None,
                     [r_pb[bank], r_const], [r_uT[c]])
            else:
                P.act(uT[:, c, :], pT[:, 0:TM], AF.Copy, [r_pb[bank], r_const], [r_uT[c]],
                      scale=colp[:, CA_G + c:CA_G + c + 1])
        for fc in range(10):
            bank = 4 + (fc % 2)
            for k in range(8):
                P.mm(pbk(bank), wA[:, k, fc * 128:(fc + 1) * 128], uT[:, k, :], k == 0, k == 7,
                     [res("wA"), r_uT[k]], [r_pb[bank]])
            if fc < 2:
                P.act(QT[:, fc, :], pbk(bank), AF.Copy, [r_pb[bank]], [res(f"QT{fc}")], scale=0.125)
            elif fc < 4:
                P.copy("dve", KT[:, fc - 2, j * TM:(j + 1) * TM], pbk(bank), [r_pb[bank]], [r_KT[fc - 2][j]])
            elif fc < 6:
                P.act(hq[:, fc - 4, :], pbk(bank), AF.Copy, [r_pb[bank]], [res(f"hq{fc - 4}")])
            elif fc < 8:
                P.act(hsgs[:, fc - 6, :], pbk(bank), AF.Sigmoid, [r_pb[bank]], [res(f"hsg{fc - 6}")])
            else:
                P.copy("dve", hgt[:, fc - 8, :], pbk(bank), [r_pb[bank]], [res(f"hgt{fc - 8}")])
                P.act(gsig[:, fc - 8, :], pbk(bank), AF.Sigmoid, [r_pb[bank]], [res(f"gsig{fc - 8}")])
        for s in range(4):
            bank = 4 + s % 2
            for k in range(8):
                P.mm(pbk(bank), uT[:, k, s * 128:(s + 1) * 128], wA[:, k, 1280:1792], k == 0, k == 7,
                     [res("wA"), r_uT[k]], [r_pb[bank]])
            P.copy("dve", V[:, 4 * j + s, :], pbk(bank)[:, 0:256], [r_pb[bank]], [r_V[j]])
            P.act(hi[:, s, :], pbk(bank)[:, 256:512], AF.Copy, [r_pb[bank]], [res(f"hi{s}")])

        def hgrn_thunks(hd):
            rt = lambda n: res(f"hg_{n}")
            hsg = hsgs[:, hd, :]
            r_sg = res(f"hsg{hd}")
            gsl = gsig[:, hd, :]
            r_gs = res(f"gsig{hd}")
            OB = 3
            SB_ = 2
            th = []

            def prep1():
                P.ts("dve", hff[:], hsg, oml[:, hd:hd + 1], lb[:, hd:hd + 1], ALU.mult, ALU.add,
                     [r_sg, res("lb"), res("lb2")], [rt("f")])
                P.act(hg_[:], hff[:], AF.Ln, [rt("f")], [rt("g")])
                P.ts("pool", hkk[:], hff[:], -1.0, 1.0, ALU.mult, ALU.add, [rt("f")], [rt("kk")])
                P.op("dve", lambda e: e.tensor_tensor_scan(out=hG[:], data0=mres[:], data1=hg_[:], initial=0.0,
                                                           op0=ALU.mult, op1=ALU.add),
                     [rt("g"), r_const], [rt("G")])
                P.stt(gsl, hgt[:, hd, :], colp[:, CA_NG + hd:CA_NG + hd + 1], gsl, ALU.mult, ALU.mult,
                      [res(f"hgt{hd}"), r_gs, r_const], [r_gs])
            th.append(prep1)

            def prep2():
                P.act(heG[:], hG[:], AF.Exp, [rt("G")], [rt("eG")])
                P.act(hsg, hG[:], AF.Exp, [rt("G"), rt("f")], [r_sg], scale=-1.0)
                P.tt("dve", hqd[:], hq[:, hd, :], heG[:], ALU.mult, [res(f"hq{hd}"), rt("eG")], [rt("qd")])
                P.tt("pool", hg_[:], hkk[:], hsg, ALU.mult, [rt("kk"), r_sg, rt("G")], [rt("g")])
                P.copy("pool", hkib[:], hg_[:], [rt("g")], [rt("kib")])
            th.append(prep2)

            def prep3():
                for ch in range(16):
                    P.ts("pool", hks[:, ch * 32:(ch + 1) * 32], hg_[:, ch * 32:(ch + 1) * 32],
                         heG[:, ch * 32 + 31:ch * 32 + 32], None, ALU.mult, None, [rt("g"), rt("eG")], [rt("ks")])
                pT = pbk(7).bitcast(BF16)
                for s in range(4):
                    P.tr(pT[:, s * 128:(s + 1) * 128], hks[:, s * 128:(s + 1) * 128], ident, [rt("ks"), r_const],
                         [r_pb[7]])
                P.copy("dve", kstm[:].rearrange("p s d -> p (s d)"), pT[:, 0:TM], [r_pb[7]], [rt("kstm")])
            th.append(prep3)

            def block_head(s):
                tsl = slice(s * 128, (s + 1) * 128)
                P.mm(pbk(SB_)[:, 0:128], hkib[:, tsl], hqd[:, tsl], True, True, [rt("kib"), rt("qd")], [r_pb[SB_]])
                P.tt("dve", smk[s % 2][:], pbk(SB_)[:, 0:128], maskH, ALU.mult, [r_pb[SB_], r_const],
                     [rt(f"smk{s % 2}")])
                P.mm(pbk(OB)[:, tsl], hi[:, s, hd * 128:(hd + 1) * 128], smk[s % 2][:], True, False,
                     [res(f"hi{s}"), rt(f"smk{s % 2}")], [r_pb[OB]])

            def chunk_step(s, cc):
                n = hstep[hd]
                cur, nxt = n % 2, (n + 1) % 2
                ch = s * 4 + cc
                csl = slice(s * 128 + cc * 32, s * 128 + (cc + 1) * 32)
                P.mm(pbk(OB)[:, csl], Sbf[hd][cur][:], hqd[:, csl], False, cc == 3,
                     [res(f"Sb{hd}_{cur}"), rt("qd")], [r_pb[OB]])
                psl = slice(cc * 32, (cc + 1) * 32)
                P.op("pe", lambda e: e.matmul(
                    pbk(SB_)[:, 128:256], lhsT=kstm[psl, s, :], rhs=hi[psl, s, hd * 128:(hd + 1) * 128],
                    start=True, stop=True, tile_position=(cc * 32, 0)),
                     [rt("kstm"), res(f"hi{s}")], [r_pb[SB_]])
                P.stt(Sst[hd][nxt][:], Sst[hd][cur][:], heG[:, ch * 32 + 31:ch * 32 + 32], pbk(SB_)[:, 128:256],
                      ALU.mult, ALU.add, [res(f"S{hd}_{cur}"), rt("eG"), r_pb[SB_]], [res(f"S{hd}_{nxt}")])
                P.copy("pool", Sbf[hd][nxt][:], Sst[hd][nxt][:], [res(f"S{hd}_{nxt}")], [res(f"Sb{hd}_{nxt}")])
                hstep[hd] += 1

            for s in range(4):
                th.append((lambda s: lambda: block_head(s))(s))
                for cc in range(4):
                    th.append((lambda s, cc: lambda: chunk_step(s, cc))(s, cc))

            def fin1():
                P.act(osq[:], pbk(OB), AF.Square, [r_pb[OB]], [rt("osq")])
                P.mm(pbk(SB_), ones128, osq[:], True, True, [r_const, rt("osq")], [r_pb[SB_]])
                P.act(hrs[:], pbk(SB_), AF.Ln, [r_pb[SB_]], [rt("rs")], bias=RMS_EPS)
                P.act(hrs[:], hrs[:], AF.Exp, [rt("rs")], [rt("rs")], scale=-0.5)
            th.append(fin1)

            def fin2():
                P.tt("dve", ho1[:], pbk(OB), hrs[:], ALU.mult, [r_pb[OB], rt("rs")], [rt("o1")])
                P.tt("pool", ohT[hd][:], ho1[:], gsl, ALU.mult, [rt("o1"), r_gs], [res(f"ohT{hd}")])
                outs.append(P.dma("pool", out_fn(2 + hd, j), ohT[hd][:], reads=[res(f"ohT{hd}")], writes=out_w))
            th.append(fin2)
            return th

        if do_hg and not do_attn:
            for hd in range(2):
                for f_ in hgrn_thunks(hd):
                    f_()

        if do_attn:
            for p in range(2):
                cb = (4, 5)
                ob = 6
                for hh in range(2):
                    P.mm(pbk(cb[hh]), zeroM, uT[:, 0, :], True, False, [r_const, r_uT[0]], [r_pb[cb[hh]]])
                P.mm(pbk(ob), zeroM, uT[:, 0, :], True, False, [r_const, r_uT[0]], [r_pb[ob]])
                nkb = 4 * j + 4
                kbs = list(range(nkb - 1, -1, -1))
                extra = hgrn_thunks(p) if do_hg else []
                per_it = (len(extra) + nkb - 1) // nkb

                def q0_of(kb):
                    m = kb - 4 * j
                    return 128 * m if m > 0 else 0

                def st_qk(t):
                    kb = kbs[t]
                    q0 = q0_of(kb)
                    zi = 0
                    ksl = slice(kb * 128, (kb + 1) * 128)
                    for hh in range(2):
                        prt = slice(hh * 64, (hh + 1) * 64)
                        P.mm(pbk(2 * zi + hh)[:, q0:TM], KT[prt, p, ksl], QT[prt, p, q0:TM], True, True,
                             [r_KT[p][kb // 4], res(f"QT{p}")], [r_pb[2 * zi + hh]])

                def st_act(t):
                    kb = kbs[t]
                    q0 = q0_of(kb)
                    zi, e3, i2 = 0, t % 3, t % 2
                    P.act(eZ[e3][:, :, q0:TM], pzz[zi][:, :, q0:TM], AF.Exp, [r_pb[2 * zi], r_pb[2 * zi + 1]],
                          [res(f"eZ{e3}")])
                    if kb - 4 * j >= 0:
                        for hh in range(2):
                            P.tt("pool", eZ[e3][:, hh, q0:q0 + 128], eZ[e3][:, hh, q0:q0 + 128], maskD, ALU.mult,
                                 [res(f"eZ{e3}"), r_const], [res(f"eZ{e3}")])
                    P.act(spb[i2][:, :, q0:TM], eZ[e3][:, :, q0:TM], AF.Ln, [res(f"eZ{e3}")], [res(f"sp{i2}")],
                          bias=1.0)

                def st_negT(t):
                    q0 = q0_of(kbs[t])
                    i2 = t % 2
                    for hh in range(2):
                        P.mm(pbk(cb[hh])[:, q0:TM], negT, spb[i2][:, hh, q0:TM], False, False,
                             [r_const, res(f"sp{i2}")], [r_pb[cb[hh]]])

                def st_expc(t):
                    q0 = q0_of(kbs[t])
                    i2, e3 = t % 2, t % 3
                    P.act(Xb[i2][:, :, q0:TM], pc[:, :, q0:TM], AF.Exp, [r_pb[4], r_pb[5]], [res(f"X{i2}")])
                    P.tt("dve", Wb[i2][:, :, q0:TM], eZ[e3][:, :, q0:TM], Xb[i2][:, :, q0:TM], ALU.mult,
                         [res(f"eZ{e3}"), res(f"X{i2}")], [res(f"W{i2}")])

                def st_negU(t):
                    q0 = q0_of(kbs[t])
                    i2 = t % 2
                    for hh in range(2):
                        P.mm(pbk(cb[hh])[:, q0:TM], negU, spb[i2][:, hh, q0:TM], False, False,
                             [r_const, res(f"sp{i2}")], [r_pb[cb[hh]]])

                def st_pv(t):
                    kb = kbs[t]
                    q0 = q0_of(kb)
                    i2 = t % 2
                    for hh in range(2):
                        P.mm(pbk(ob)[hh * 64:(hh + 1) * 64, q0:TM],
                             V[:, kb, p * 128 + hh * 64:p * 128 + (hh + 1) * 64],
                             Wb[i2][:, hh, q0:TM], False, kb == 0, [r_V[kb // 4], res(f"W{i2}")], [r_pb[ob]])

                n = nkb
                st_qk(0)
                st_act(0)
                if n > 1:
                    st_qk(1)
                    st_act(1)
                for t in range(n):
                    st_negT(t)
                    if t + 2 < n:
                        st_qk(t + 2)
                    st_expc(t)
                    if t + 1 < n:
                        st_negU(t)
                    if t >= 1:
                        st_pv(t - 1)
                    if t + 2 < n:
                        st_act(t + 2)
                    for _ in range(per_it):
                        if extra:
                            extra.pop(0)()
                st_pv(n - 1)
                while extra:
                    extra.pop(0)()
                P.act(osb[p][:], pbk(ob), AF.Copy, [r_pb[ob]], [res(f"osb{p}")])
                outs.append(P.dma("pool", out_fn(p, j), osb[p][:], reads=[res(f"osb{p}")], writes=out_w))


def make_phase_a_nc(ntiles=16, do_attn=True, do_hg=True):
    nc = bass.Bass("TRN2", target_bir_lowering=False)
    io = {
        "xA": nc.dram_tensor("xA", [SEQ, D], F32, kind="ExternalInput").ap(),
        "wA32": nc.dram_tensor("wA32", [128, 8, WA_COLS], F32, kind="ExternalInput").ap(),
        "colpA": nc.dram_tensor("colpA", [128, NCOLA], F32, kind="ExternalInput").ap(),
        "constA": nc.dram_tensor("constA", [128, NCA, 128], F32, kind="ExternalInput").ap(),
        "mres": nc.dram_tensor("mres", [128, 512], F32, kind="ExternalInput").ap(),
        "oTA": nc.dram_tensor("oTA", [128, 4, SEQ], BF16, kind="ExternalOutput").ap(),
    }
    P = Prog(nc)
    with contextlib.ExitStack() as es:
        C = Ctx(nc, es)
        build_phase_a(nc, P, C, io, ntiles, do_attn, do_hg)
        fin = P.op("sp", lambda e: e.nop(), (), ())
        fin.deps.extend(io["_outs"])
        P.finish(es)
    return nc, P, C


def phase_a_inputs(inp, b, g):
    w = inp["w_in_ab"][0]
    cols = []
    for base in (0, 512):
        cols.append(w[:, base + 256 * g: base + 256 * g + 256])
    for base in (1536, 2048, 3072):
        cols.append(w[:, base + 256 * g: base + 256 * g + 256])
    cols.append(w[:, 1024 + 256 * g: 1024 + 256 * g + 256])
    cols.append(w[:, 2560 + 256 * g: 2560 + 256 * g + 256])
    wsel = np.concatenate(cols, axis=1)
    wA32 = np.ascontiguousarray(wsel.reshape(8, 128, WA_COLS).transpose(1, 0, 2))
    colp = np.zeros((128, NCOLA), np.float32)
    colp[:, CA_G:CA_G + 8] = colvec(inp["norm_mix_g"][0])
    lg = inp["hg_lb_logits"]
    for hd in range(2):
        head = 2 * g + hd
        colp[:, CA_LOG + hd * 3: CA_LOG + hd * 3 + 3] = lg[:, head * 128:(head + 1) * 128].T
        colp[:, CA_NG + hd] = inp["hg_norm_g"][0][head]
    return {"xA": np.ascontiguousarray(inp["x"][b]), "wA32": wA32, "colpA": colp}


def phase_a_consts():
    cst = np.zeros((128, NCA, 128), np.float32)
    jj = np.arange(128)[:, None]
    ss = np.arange(128)[None, :]
    cst[:, KA_IDENT, :] = np.eye(128)
    cst[:, KA_NEGT, :] = -1.0 * (jj >= ss)
    cst[:, KA_NEGU, :] = -1.0 * (jj < ss)
    cst[:, KA_MASKD, :] = (jj < ss)
    cst[:, KA_MASKH, :] = (jj <= ss) & ((jj // 32) == (ss // 32))
    cst[:, KA_ONES, :] = 1.0 / 128.0
    mres = np.ones((128, 512), np.float32)
    mres[:, ::32] = 0.0
    return cst, mres


GCOLS = 128 + SEQ
PAIRS = [[0, 1], [2, 3], [4, 5], [6, 7]]


def make_fused_nc(ntiles_a=16, tiles_b=None):
    nc = bass.Bass("TRN2", target_bir_lowering=False)
    if tiles_b is None:
        tiles_b = [(0, 1, False)] + [(128 + 512 * i, 4, True) for i in range(8)]
    io = {
        "xA": nc.dram_tensor("xA", [SEQ, D], F32, kind="ExternalInput").ap(),
        "wA32": nc.dram_tensor("wA32", [128, 8, WA_COLS], F32, kind="ExternalInput").ap(),
        "colpA": nc.dram_tensor("colpA", [128, NCOLA], F32, kind="ExternalInput").ap(),
        "constA": nc.dram_tensor("constA", [128, NCA, 128], F32, kind="ExternalInput").ap(),
        "mres": nc.dram_tensor("mres", [128, 512], F32, kind="ExternalInput").ap(),
        "xB": nc.dram_tensor("xB", [NTB, D], F32, kind="ExternalInput").ap(),
        "w32": nc.dram_tensor("w32", [NWT_B, 128, 4096], F32, kind="ExternalInput").ap(),
        "w16": nc.dram_tensor("w16", [NWT_B, 128, 4096], BF16).ap(),
        "colp": nc.dram_tensor("colp", [128, NCOLP], F32, kind="ExternalInput").ap(),
        "rowp": nc.dram_tensor("rowp", [128, 2, D], F32, kind="ExternalInput").ap(),
        "ident": nc.dram_tensor("ident", [128, 128], F32, kind="ExternalInput").ap(),
        "out": nc.dram_tensor("out", [NTB - 128, D], F32, kind="ExternalOutput").ap(),
    }
    cc_src = [nc.dram_tensor(f"cc_src{j}", [512, 512], BF16) for j in range(16)]
    cc_dst = [nc.dram_tensor(f"cc_dst{j}", [1024, 512], BF16) for j in range(16)]
    cc_pad = nc.dram_tensor("cc_pad", [1024, 128], BF16)
    P = Prog(nc)
    with contextlib.ExitStack() as es0:
        P.alloc_sems(es0)
        r_dst = Res("cc_dst")
        pre = precast_ops(io)
        pre_res = {}
        per_tile = (len(pre) + ntiles_a - 1) // ntiles_a
        state = {"ccops": []}

        def gather_tile(j):
            deps = io["_outs"][-4:]
            op = P.cc((lambda j: lambda e: e.collective_compute(
                "AllGather", ALU.bypass, replica_groups=PAIRS,
                ins=[cc_src[j].ap().opt()], outs=[cc_dst[j].ap().opt()]))(j), [], [])
            op.deps.extend(deps)
            state["ccops"].append(op)

        def hook(j):
            if j > 0:
                gather_tile(j - 1)
            for a in pre[j * per_tile:(j + 1) * per_tile]:
                r = pre_res.setdefault(a[2], Res(a[2]))
                P.dma("pool", a[0], a[1], writes=[r])

        with contextlib.ExitStack() as esA:
            C = Ctx(nc, esA)
            build_phase_a(nc, P, C, io, ntiles_a,
                          out_fn=lambda q, j: cc_src[j].ap()[q * 128:(q + 1) * 128, :],
                          tile_hook=hook)
            gather_tile(ntiles_a - 1)
            zpad = [P.dma("pool", cc_pad.ap()[q * 128:(q + 1) * 128, :], io["_zeroM"]) for q in range(8)]
            io["_outs"] = []
            fin_cc = P.op("pool", lambda e: e.nop(), (), [r_dst])
            fin_cc.deps.extend(state["ccops"][-1:] + zpad)
            P.barrier()
            P.emit_segment()
            sbA = C.sb_bytes
        with contextlib.ExitStack() as esB:
            C = Ctx(nc, esB, "b_")

            def gsrc(half, tok0, T):
                t = 4096 * half - 128 + tok0
                if t < 0:
                    return cc_pad.ap().rearrange("(k p) t -> p k t", p=128)[:, :, 0:T]
                j, c0 = divmod(t, 512)
                return cc_dst[j].ap().rearrange("(k p) t -> p k t", p=128)[:, :, c0:c0 + T]

            build_phase_b(nc, P, C, io, tiles_b, precast_done=pre_res, gathered=gsrc, gathered_res=r_dst)
            fin = P.op("sp", lambda e: e.nop(), (), ())
            fin.deps.extend(io["_outs"])
            P.emit_segment()
            sbB = C.sb_bytes
    P.sb_bytes = (sbA, sbB)
    return nc, P


def wtile_kn(W, col0, ncols=512):
    return np.ascontiguousarray(W[:, col0:col0 + ncols].reshape(8, 128, ncols).transpose(1, 0, 2)).reshape(128, -1)


def phase_b_weights(inp, w_out_perm):
    tiles = []
    for half in range(2):
        tiles.append(wtile_kn(w_out_perm, half * 512))
    for layer in range(2):
        pass
    w1 = inp["w_ff1"]
    w2 = inp["w_ff2"]
    wg = inp["conv_w_glu"][0]
    wp = inp["conv_w_pw"][0]

    def ff1_tiles(l):
        return [wtile_kn(w1[l], j * 512) for j in range(8)]

    def ff2_tiles(l):
        out = []
        for half in range(2):
            for cg in range(4):
                blk = w2[l][cg * 1024:(cg + 1) * 1024, half * 512:(half + 1) * 512]
                out.append(np.ascontiguousarray(blk.reshape(8, 128, 512).transpose(1, 0, 2)).reshape(128, -1))
        return out

    tiles += ff1_tiles(0) + ff2_tiles(0)
    for j in range(4):
        cols = []
        for q in range(2):
            cols.append(wg[:, (2 * j + q) * 128:(2 * j + q + 1) * 128])
        for q in range(2):
            cols.append(wg[:, 1024 + (2 * j + q) * 128:1024 + (2 * j + q + 1) * 128])
        blk = np.concatenate(cols, axis=1)
        tiles.append(wtile_kn(blk, 0))
    for half in range(2):
        tiles.append(wtile_kn(wp, half * 512))
    tiles += ff1_tiles(1) + ff2_tiles(1)
    assert len(tiles) == NWT_B
    return np.stack(tiles).astype(np.float32)


def colvec(v):
    return np.ascontiguousarray(v.reshape(8, 128).T)


def phase_b_params(inp, halo_flag):
    colp = np.zeros((128, NCOLP), np.float32)
    colp[:, CP_G_FFN0:CP_G_FFN0 + 8] = colvec(inp["norm_ffn_g"][0])
    colp[:, CP_G_MIX1:CP_G_MIX1 + 8] = colvec(inp["norm_mix_g"][1])
    colp[:, CP_G_FFN1:CP_G_FFN1 + 8] = colvec(inp["norm_ffn_g"][1])
    colp[:, CP_BA:CP_BA + 8] = colvec(inp["conv_b_glu"][0][:1024])
    colp[:, CP_BB:CP_BB + 8] = colvec(inp["conv_b_glu"][0][1024:])
    colp[:, CP_BDW:CP_BDW + 8] = colvec(inp["conv_b_dw"][0])
    colp[:, CP_LNG:CP_LNG + 8] = colvec(inp["conv_ln_g"][0])
    colp[:, CP_LNB:CP_LNB + 8] = colvec(inp["conv_ln_b"][0])
    colp[:, CP_HALO] = halo_flag
    colp[:, CP_F0] = 1.0 - halo_flag
    colp[:, CP_F1] = halo_flag
    wdw = inp["conv_w_dw"][0]
    for c in range(8):
        colp[:, CP_WDW + c * CONVW: CP_WDW + (c + 1) * CONVW] = wdw[:, c * 128:(c + 1) * 128].T
    rowp = np.zeros((128, 2, D), np.float32)
    rowp[:, 0, :] = inp["conv_b_pw"][0][None, :]
    rowp[:, 1, :] = inp["final_norm_g"][None, :]
    return colp, rowp


def w_out_permuted(inp):
    w = inp["w_out_ab"][0]
    rows = []
    for gg in range(2):
        rows.append(w[256 * gg: 256 * gg + 256])
        rows.append(w[512 + 256 * gg: 512 + 256 * gg + 256])
    return np.concatenate(rows, axis=0)


def kernel(**inputs):
    inp = {k: np.asarray(v) for k, v in inputs.items()}
    cores = list(range(NCORES))
    nc, _ = make_fused_nc()
    cst, mres = phase_a_consts()
    w32 = phase_b_weights(inp, w_out_permuted(inp))
    ident = np.eye(128, dtype=np.float32)
    maps = []
    for c in cores:
        b, g = divmod(c, 2)
        m = phase_a_inputs(inp, b, g)
        m["constA"] = cst
        m["mres"] = mres
        colp, rowp = phase_b_params(inp, float(g))
        xB = np.zeros((NTB, D), np.float32)
        t0 = 4096 * g - 128
        lo = max(t0, 0)
        xB[lo - t0:] = inp["x"][b, lo:t0 + NTB]
        m.update({"xB": xB, "w32": w32, "colp": colp, "rowp": rowp, "ident": ident})
        maps.append(m)
    res = run_bass_kernel_spmd(nc, maps, core_ids=cores)
    out = np.zeros((BATCH, SEQ, D), np.float32)
    for c in cores:
        b, g = divmod(c, 2)
        out[b, 4096 * g:4096 * (g + 1)] = np.asarray(res.results[c]["out"])
    return out
```

```python
import contextlib
import numpy as np
import ml_dtypes
import concourse.bass as bass
import concourse.mybir as mybir
from concourse.bass_utils import run_bass_kernel_spmd

F32 = mybir.dt.float32
BF16 = mybir.dt.bfloat16
F32R = mybir.dt.float32r
AF = mybir.ActivationFunctionType
ALU = mybir.AluOpType

D = 1024
DFF = 4096
SEQ = 8192
BATCH = 4
NCORES = 8
RMS_EPS = 1e-6
LN_EPS = 1e-5
CONVW = 31


class Res:
    __slots__ = ("name", "last_w", "readers", "excl")

    def __init__(self, name, excl=False):
        self.name = name
        self.last_w = None
        self.readers = []
        self.excl = excl


class Op:
    __slots__ = ("eng", "fn", "deps", "sig", "sigcount", "dma", "dsem", "dval", "prewait", "inc")

    def __init__(self, eng, fn, dma):
        self.eng = eng
        self.fn = fn
        self.deps = []
        self.sig = False
        self.sigcount = 0
        self.dma = dma
        self.dsem = None
        self.dval = 0
        self.prewait = None
        self.inc = 16


class Prog:
    ENGS = ("pe", "act", "dve", "pool", "sp")
    DMA_ENGS = ("sp", "pool", "act")

    def __init__(self, nc, ndma_sems=8):
        self.nc = nc
        self.ops = {e: [] for e in self.ENGS}
        self.ndma = {e: 0 for e in self.ENGS}
        self.K = ndma_sems
        self.all_ops = []

    def _add(self, eng, fn, reads, writes, dma):
        op = Op(eng, fn, dma)
        deps = []
        for r in reads:
            if r.last_w is not None:
                deps.append(r.last_w)
            if r.excl:
                deps.extend(x for x in r.readers if x.eng != eng)
        for w in writes:
            if w.last_w is not None:
                deps.append(w.last_w)
            deps.extend(w.readers)
        for r in reads:
            r.readers.append(op)
        for w in writes:
            w.last_w = op
            w.readers = []
        seen = set()
        for d in deps:
            if id(d) not in seen and d is not op:
                seen.add(id(d))
                op.deps.append(d)
        if dma:
            i = self.ndma[eng]
            self.ndma[eng] += 1
            op.dsem = (eng, i % self.K)
            op.dval = 16 * (i // self.K + 1)
            if i >= self.K:
                op.prewait = (op.dsem, 16 * (i // self.K))
        self.ops[eng].append(op)
        self.all_ops.append(op)
        return op

    def op(self, eng, fn, reads=(), writes=()):
        return self._add(eng, fn, reads, writes, False)

    def dma(self, eng, out, in_, reads=(), writes=()):
        return self._add(eng, lambda e: e.dma_start(out=out, in_=in_), reads, writes, True)

    def mm(self, out, lhsT, rhs, start, stop, reads, writes):
        return self.op("pe", lambda e: e.matmul(out, lhsT=lhsT, rhs=rhs, start=start, stop=stop),
                       reads, writes)

    def tr(self, out, in_, ident, reads, writes):
        return self.op("pe", lambda e: e.transpose(out=out, in_=in_, identity=ident), reads, writes)

    def act(self, out, in_, func, reads, writes, **kw):
        return self.op("act", lambda e: e.activation(out=out, in_=in_, func=func, **kw), reads, writes)

    def ts(self, eng, out, in0, s1, s2, op0, op1, reads, writes):
        if op1 is None:
            return self.op(eng, lambda e: e.tensor_scalar(out=out, in0=in0, scalar1=s1, scalar2=None, op0=op0),
                           reads, writes)
        return self.op(eng, lambda e: e.tensor_scalar(out=out, in0=in0, scalar1=s1, scalar2=s2, op0=op0, op1=op1),
                       reads, writes)

    def tt(self, eng, out, in0, in1, op, reads, writes):
        return self.op(eng, lambda e: e.tensor_tensor(out=out, in0=in0, in1=in1, op=op), reads, writes)

    def stt(self, out, in0, scalar, in1, op0, op1, reads, writes):
        return self.op("dve", lambda e: e.scalar_tensor_tensor(out=out, in0=in0, scalar=scalar, in1=in1,
                                                               op0=op0, op1=op1), reads, writes)

    def copy(self, eng, out, in_, reads, writes):
        return self.op(eng, lambda e: e.tensor_copy(out=out, in_=in_), reads, writes)

    def memset(self, eng, out, val, writes):
        return self.op(eng, lambda e: e.memset(out, val), (), writes)

    def cc(self, fn, reads, writes):
        op = self._add("pool", fn, reads, writes, False)
        op.dma = True
        self.ncc = getattr(self, "ncc", 0) + 1
        op.dsem = ("cc", 0)
        op.dval = self.ncc
        op.inc = 1
        return op

    def barrier(self):
        tails = []
        for e in self.ENGS:
            comp = [o for o in self.ops[e] if not o.dma]
            if comp:
                tails.append(comp[-1])
            dm = [o for o in self.ops[e] if o.dma and o.dsem[0] != "cc"]
            tails.extend(dm[-self.K:])
        for e in self.ENGS:
            op = self.op(e, lambda eng: eng.nop(), (), ())
            op.deps.extend(t for t in tails if t is not op)

    def _skip(self, d, op):
        return (not d.dma) and d.eng == op.eng and op.eng == "pe" and not op.dma

    def alloc_sems(self, es):
        nc = self.nc
        self.sems = {"eng": {e: es.enter_context(nc.semaphore(f"se_{e}")) for e in self.ENGS},
                     "dma": {(e, i): es.enter_context(nc.semaphore(f"sd_{e}{i}"))
                             for e in self.DMA_ENGS for i in range(self.K)}}
        self.sems["dma"][("cc", 0)] = es.enter_context(nc.semaphore("s_cc"))
        self.emitted = {e: 0 for e in self.ENGS}
        self.sigc = {e: 0 for e in self.ENGS}
        self.waited = {e: {} for e in self.ENGS}
        self.nwaits = 0

    def plan(self):
        seg = {e: self.ops[e][self.emitted[e]:] for e in self.ENGS}
        segset = set(id(o) for e in self.ENGS for o in seg[e])
        for e in self.ENGS:
            for op in seg[e]:
                for d in op.deps:
                    if d.dma or self._skip(d, op):
                        continue
                    if id(d) not in segset:
                        assert d.sig, "cross-segment dependency on a non-signalling op"
                    d.sig = True
        for e in self.ENGS:
            for op in seg[e]:
                if op.sig and not op.dma:
                    self.sigc[e] += 1
                    op.sigcount = self.sigc[e]
        plans = {}
        for e in self.ENGS:
            waited = self.waited[e]
            plan = []
            for op in seg[e]:
                ws = {}
                if op.prewait is not None:
                    k, v = op.prewait
                    ws[("d",) + k] = v
                for d in op.deps:
                    if d.dma:
                        k = ("d",) + d.dsem
                        v = d.dval
                    else:
                        if self._skip(d, op):
                            continue
                        k = ("e", d.eng)
                        v = d.sigcount
                    if ws.get(k, 0) < v:
                        ws[k] = v
                wl = []
                for k, v in ws.items():
                    if waited.get(k, 0) >= v:
                        continue
                    waited[k] = v
                    wl.append((k, v))
                self.nwaits += len(wl)
                plan.append((op, wl))
            plans[e] = plan
            self.emitted[e] = len(self.ops[e])
        return plans

    def run_engine(self, e, engine, plans):
        esem = self.sems["eng"]
        dsem = self.sems["dma"]
        for op, wl in plans[e]:
            for k, v in wl:
                if k[0] == "d":
                    engine.wait_ge(dsem[(k[1], k[2])], v)
                else:
                    engine.wait_ge(esem[k[1]], v)
            ins = op.fn(engine)
            if op.dma:
                ins.then_inc(dsem[op.dsem], getattr(op, "inc", 16))
            elif op.sig:
                ins.then_inc(esem[e], 1)

    def emit_segment(self):
        plans = self.plan()
        P = self
        with self.nc.Block() as block:
            @block.sync
            def _(e):
                P.run_engine("sp", e, plans)

            @block.tensor
            def _(e):
                P.run_engine("pe", e, plans)

            @block.scalar
            def _(e):
                P.run_engine("act", e, plans)

            @block.vector
            def _(e):
                P.run_engine("dve", e, plans)

            @block.gpsimd
            def _(e):
                P.run_engine("pool", e, plans)

    def finish(self, es):
        self.alloc_sems(es)
        self.emit_segment()


class Ctx:
    def __init__(self, nc, es, prefix=""):
        self.nc = nc
        self.es = es
        self.sb_bytes = 0
        self.prefix = prefix

    def sb(self, name, shape, dt):
        n = 1
        for s in shape[1:]:
            n *= s
        self.sb_bytes += n * (4 if dt in (F32, F32R) else 2)
        return self.es.enter_context(self.nc.sbuf_tensor("s_" + self.prefix + name, shape, dt))

    def ps(self, name):
        return self.es.enter_context(self.nc.psum_tensor("p_" + self.prefix + name, [128, 512], F32))


NTB = 4096 + 128
NWT_B = 40
CP_G_FFN0 = 0
CP_G_MIX1 = 8
CP_G_FFN1 = 16
CP_BA = 24
CP_BB = 32
CP_BDW = 40
CP_LNG = 48
CP_LNB = 56
CP_HALO = 64
CP_F0 = 65
CP_F1 = 66
CP_WDW = 72
NCOLP = CP_WDW + 8 * CONVW


def precast_ops(io):
    w32, w16 = io["w32"], io["w16"]
    out = []
    for i in range(w32.shape[0]):
        for hh in range(2):
            out.append((w16[i, :, hh * 2048:(hh + 1) * 2048], w32[i, :, hh * 2048:(hh + 1) * 2048], f"w16_{i}_{hh}"))
    return out


def build_phase_b(nc, P, C, io, tiles, precast_done=None, gathered=None, gathered_res=None):
    xB, w32, w16, colp_d, rowp_d, ident_d, outd = (io[k] for k in
                                                  ("xB", "w32", "w16", "colp", "rowp", "ident", "out"))
    oTd = io.get("oT")
    NW = 5
    TM = 512
    h = C.sb("h", [128, 4, D], F32)
    utm = C.sb("utm", [128, 4, D], BF16)
    uT = C.sb("uT", [128, 8, TM], BF16)
    hidT = C.sb("hidT", [128, 32, TM], BF16)
    wb = [C.sb(f"wb{i}", [128, 4096], BF16) for i in range(NW)]
    oTt = C.sb("oTt", [128, 8, TM], BF16)
    if gathered is not None:
        oTa = C.sb("oTa", [128, 8, TM], BF16)
        oTb = C.sb("oTb", [128, 8, TM], BF16)
    aT = C.sb("aT", [128, 8, 32 + TM], BF16)
    ybuf = C.sb("ybuf", [128, 8, TM], F32)
    zT = C.sb("zT", [128, 8, TM], BF16)
    sig = [C.sb(f"sig{i}", [128, TM], F32) for i in range(2)]
    rtmp = [C.sb(f"rtmp{i}", [128, TM], BF16) for i in range(2)]
    junk = C.sb("junk", [128, D], BF16)
    ysq = C.sb("ysq", [128, TM], F32)
    lnm = C.sb("lnm", [128, TM], F32)
    lnm2 = C.sb("lnm2", [128, TM], F32)
    lnr = C.sb("lnr", [128, TM], F32)
    lnt = [C.sb(f"lnt{i}", [128, TM], F32) for i in range(2)]
    ssq = C.sb("ssq", [128, 4], F32)
    rstd = C.sb("rstd", [128, 4], F32)
    colp = C.sb("colp", [128, NCOLP], F32)
    rowp = C.sb("rowp", [128, 2, D], F32)
    ident = C.sb("ident", [128, 128], BF16)
    ones_r = C.sb("ones_r", [128, 128], F32)
    pb = [C.ps(f"pb{i}") for i in range(8)]

    R = {}

    def res(n):
        if n not in R:
            R[n] = Res(n)
        return R[n]

    r_h = [res(f"h{s}") for s in range(4)]
    r_utm = [res(f"utm{s}") for s in range(4)]
    r_uT = [res(f"uT{c}") for c in range(8)]
    r_hid = [res(f"hid{c}") for c in range(32)]
    r_wb = [res(f"wb{i}") for i in range(NW)]
    r_pb = [R.setdefault(f"pb{i}", Res(f"pb{i}", True)) for i in range(8)]
    r_aT = [res(f"aT{c}") for c in range(8)]
    r_y = [res(f"y{c}") for c in range(8)]
    r_zT = [res(f"zT{c}") for c in range(8)]
    r_sig = [res("sig0"), res("sig1")]
    r_rt = [res("rt0"), res("rt1")]
    r_lnt = [res("lnt0"), res("lnt1")]
    r_const = res("const")
    r_w16 = [res(f"w16_{i}") for i in range(NWT_B)]

    P.dma("sp", colp[:], colp_d[:, :], writes=[r_const])
    P.dma("sp", rowp[:], rowp_d[:, :, :], writes=[r_const])
    P.dma("pool", ident[:], ident_d[:, :], writes=[r_const])
    P.memset("pool", ones_r[:], 1.0 / D, writes=[r_const])
    for c in range(8):
        P.memset("pool", aT[:, c, 0:32], 0.0, writes=[r_aT[c]])

    if precast_done is None:
        for op_args in precast_ops(io):
            P.dma("pool", op_args[0], op_args[1], writes=[res(op_args[2])])
    else:
        for n, r in precast_done.items():
            R[n] = r

    wstate = {"n": 0}

    def wload(tile_idx, src=None, src_res=None):
        i = wstate["n"] % NW
        wstate["n"] += 1
        if src is None:
            P.dma("sp", wb[i][:], w16[tile_idx, :, :], reads=[res(f"w16_{tile_idx}_0"), res(f"w16_{tile_idx}_1")],
                  writes=[r_wb[i]])
        else:
            P.dma("sp", wb[i][:], src, reads=src_res, writes=[r_wb[i]])
        return wb[i], r_wb[i]

    wdg = io["wdg"]
    for c in range(8):
        cc4 = c % 4
        stg = hidT[:, 8 * cc4:8 * cc4 + 8, :].rearrange("p a b -> p (a b)")
        rs = [r_hid[8 * cc4 + q] for q in range(8)]
        for k in range(CONVW):
            wcol = CP_WDW + c * CONVW + k
            if (c + k) % 2 == 0:
                P.ts("dve", stg[:, k * 128:(k + 1) * 128], ident[:], colp[:, wcol:wcol + 1], None, ALU.mult, None,
                     [r_const] + rs, rs)
            else:
                P.act(stg[:, k * 128:(k + 1) * 128], ident[:], AF.Copy, [r_const] + rs, rs,
                      scale=colp[:, wcol:wcol + 1])
        P.dma("pool", wdg[c, :, 0:CONVW * 128], stg[:, 0:CONVW * 128], reads=rs, writes=[res(f"wdg{c}")])

    def rmsnorm_to_uT(nsub, gcol):
        T = nsub * 128
        for s in range(nsub):
            P.act(junk[:], h[:, s, :], AF.Square, [r_h[s]], [res("junk"), res("ssq")], accum_out=ssq[:, s:s + 1])
        P.act(rstd[:, 0:nsub], ssq[:, 0:nsub], AF.Ln, [res("ssq")], [res("rstd")], scale=1.0 / D, bias=RMS_EPS)
        P.act(rstd[:, 0:nsub], rstd[:, 0:nsub], AF.Exp, [res("rstd")], [res("rstd")], scale=-0.5)
        for s in range(nsub):
            P.ts("dve", utm[:, s, :], h[:, s, :], rstd[:, s:s + 1], None, ALU.mult, None,
                 [r_h[s], res("rstd")], [r_utm[s]])
        for c in range(8):
            bank = 6 + (c % 2)
            pT = pb[bank][:].bitcast(BF16)
            for s in range(nsub):
                P.tr(pT[:, s * 128:(s + 1) * 128], utm[:, s, c * 128:(c + 1) * 128], ident[:],
                     [r_utm[s], r_const], [r_pb[bank]])
            if c % 2 == 0:
                P.ts("dve", uT[:, c, 0:T], pT[:, 0:T], colp[:, gcol + c:gcol + c + 1], None, ALU.mult, None,
                     [r_pb[bank], r_const], [r_uT[c]])
            else:
                P.act(uT[:, c, 0:T], pT[:, 0:T], AF.Copy, [r_pb[bank], r_const], [r_uT[c]],
                      scale=colp[:, gcol + c:gcol + c + 1])

    def proj_tokmajor(nsub, srcT, r_src, wbase):
        for half in range(2):
            wt, rw = wload(wbase + half)
            wv = wt[:].rearrange("p (k n) -> p k n", k=8)
            for s in range(nsub):
                for k in range(8):
                    P.mm(pb[s][:, :], srcT[:, k, s * 128:(s + 1) * 128], wv[:, k, :], k == 0, k == 7,
                         [r_src[k], rw], [r_pb[s]])
                P.tt("dve", h[:, s, half * 512:(half + 1) * 512], pb[s][:, :], h[:, s, half * 512:(half + 1) * 512],
                     ALU.add, [r_pb[s], r_h[s]], [r_h[s]])

    def ffn(nsub, w1base, w2base):
        T = nsub * 128
        for j in range(8):
            wt, rw = wload(w1base + j)
            wv = wt[:].rearrange("p (k n) -> p k n", k=8)
            for cc in range(4):
                c = 4 * j + cc
                bank = 4 + (c % 2)
                for k in range(8):
                    P.mm(pb[bank][:, 0:T], wv[:, k, cc * 128:(cc + 1) * 128], uT[:, k, 0:T], k == 0, k == 7,
                         [rw, r_uT[k]], [r_pb[bank]])
                P.act(rtmp[c % 2][:, 0:T], pb[bank][:, 0:T], AF.Relu, [r_pb[bank]], [r_rt[c % 2]])
                P.tt("dve", hidT[:, c, 0:T], rtmp[c % 2][:, 0:T], pb[bank][:, 0:T], ALU.mult,
                     [r_rt[c % 2], r_pb[bank]], [r_hid[c]])
        for half in range(2):
            for cg in range(4):
                wt, rw = wload(w2base + half * 4 + cg)
                wv = wt[:].rearrange("p (k n) -> p k n", k=8)
                for s in range(nsub):
                    for cc in range(8):
                        c = cg * 8 + cc
                        P.mm(pb[s][:, :], hidT[:, c, s * 128:(s + 1) * 128], wv[:, cc, :], c == 0, c == 31,
                             [r_hid[c], rw], [r_pb[s]])
            for s in range(nsub):
                P.tt("dve", h[:, s, half * 512:(half + 1) * 512], pb[s][:, :], h[:, s, half * 512:(half + 1) * 512],
                     ALU.add, [r_pb[s], r_h[s]], [r_h[s]])

    prev_T = None
    for ti, (tok0, nsub, full) in enumerate(tiles):
        T = nsub * 128
        for s in range(nsub):
            P.dma("sp", h[:, s, :], xB[tok0 + s * 128: tok0 + (s + 1) * 128, :], writes=[r_h[s]])
        r_oTt = [res(f"oTt{k}") for k in range(8)]
        if gathered is None:
            P.dma("sp", oTt[:, :, 0:T], oTd[:, :, tok0:tok0 + T], writes=r_oTt)
        else:
            P.dma("sp", oTa[:, :, 0:T], gathered(0, tok0, T), reads=[gathered_res], writes=[res("oTa")])
            P.dma("sp", oTb[:, :, 0:T], gathered(1, tok0, T), reads=[gathered_res], writes=[res("oTb")])
            P.ts("pool", oTb[:, :, 0:T], oTb[:, :, 0:T], colp[:, CP_F1:CP_F1 + 1], None, ALU.mult, None,
                 [res("oTb"), r_const], [res("oTb")])
            P.stt(oTt[:, :, 0:T], oTa[:, :, 0:T], colp[:, CP_F0:CP_F0 + 1], oTb[:, :, 0:T], ALU.mult, ALU.add,
                  [res("oTa"), res("oTb"), r_const], r_oTt)
        proj_tokmajor(nsub, oTt, [res(f"oTt{k}") for k in range(8)], 0)
        rmsnorm_to_uT(nsub, CP_G_FFN0)
        ffn(nsub, 2, 10)
        rmsnorm_to_uT(nsub, CP_G_MIX1)
        if prev_T is not None:
            for c in range(8):
                if ti == 1:
                    P.ts("pool", aT[:, c, 2:32], aT[:, c, 2 + prev_T:32 + prev_T], colp[:, CP_HALO:CP_HALO + 1], None,
                         ALU.mult, None, [r_aT[c], r_const], [r_aT[c]])
                else:
                    P.copy("pool", aT[:, c, 2:32], aT[:, c, 2 + prev_T:32 + prev_T], [r_aT[c]], [r_aT[c]])
        for j in range(4):
            wt, rw = wload(18 + j)
            wv = wt[:].rearrange("p (k n) -> p k n", k=8)
            for q in range(2):
                c = 2 * j + q
                ba_, bb_ = (4, 5) if c % 2 == 0 else (6, 7)
                for k in range(8):
                    P.mm(pb[ba_][:, 0:T], wv[:, k, q * 128:(q + 1) * 128], uT[:, k, 0:T], k == 0, k == 7,
                         [rw, r_uT[k]], [r_pb[ba_]])
                for k in range(8):
                    P.mm(pb[bb_][:, 0:T], wv[:, k, (2 + q) * 128:(3 + q) * 128], uT[:, k, 0:T], k == 0, k == 7,
                         [rw, r_uT[k]], [r_pb[bb_]])
                P.act(sig[c % 2][:, 0:T], pb[bb_][:, 0:T], AF.Sigmoid, [r_pb[bb_], r_const], [r_sig[c % 2]],
                      bias=colp[:, CP_BB + c:CP_BB + c + 1])
                P.stt(aT[:, c, 32:32 + T], pb[ba_][:, 0:T], colp[:, CP_BA + c:CP_BA + c + 1], sig[c % 2][:, 0:T],
                      ALU.add, ALU.mult, [r_pb[ba_], r_sig[c % 2], r_const], [r_aT[c]])
        prev_T = T
        if not full:
            continue
        for c in range(8):
            wt, rw = wload(None, wdg[c, :, :], [res(f"wdg{c}")])
            bank = 4 + (c % 2)
            for k in range(CONVW):
                P.mm(pb[bank][:, 0:T], wt[:, k * 128:(k + 1) * 128], aT[:, c, 2 + k:2 + k + T], k == 0,
                     k == CONVW - 1, [rw, r_aT[c]], [r_pb[bank]])
            P.act(ybuf[:, c, 0:T], pb[bank][:, 0:T], AF.Identity, [r_pb[bank], r_const], [r_y[c]],
                  bias=colp[:, CP_BDW + c:CP_BDW + c + 1])
        for c in range(8):
            P.mm(pb[6][:, 0:T], ones_r[:], ybuf[:, c, 0:T], c == 0, c == 7,
                 [r_const, r_y[c]], [r_pb[6]])
        for c in range(8):
            P.act(ysq[:, 0:T], ybuf[:, c, 0:T], AF.Square, [r_y[c]], [res("ysq")])
            P.mm(pb[7][:, 0:T], ones_r[:], ysq[:, 0:T], c == 0, c == 7,
                 [r_const, res("ysq")], [r_pb[7]])
        P.act(lnm[:, 0:T], pb[6][:, 0:T], AF.Identity, [r_pb[6]], [res("lnm")])
        P.act(lnm2[:, 0:T], pb[6][:, 0:T], AF.Square, [r_pb[6]], [res("lnm2")])
        P.tt("dve", lnr[:, 0:T], pb[7][:, 0:T], lnm2[:, 0:T], ALU.subtract, [r_pb[7], res("lnm2")], [res("lnr")])
        P.act(lnr[:, 0:T], lnr[:, 0:T], AF.Ln, [res("lnr")], [res("lnr")], bias=LN_EPS)
        P.act(lnr[:, 0:T], lnr[:, 0:T], AF.Exp, [res("lnr")], [res("lnr")], scale=-0.5)
        for c in range(8):
            i2 = c % 2
            P.tt("pool", lnt[i2][:, 0:T], ybuf[:, c, 0:T], lnm[:, 0:T], ALU.subtract, [r_y[c], res("lnm")],
                 [r_lnt[i2]])
            P.tt("dve", lnt[i2][:, 0:T], lnt[i2][:, 0:T], lnr[:, 0:T], ALU.mult, [r_lnt[i2], res("lnr")],
                 [r_lnt[i2]])
            P.act(zT[:, c, 0:T], lnt[i2][:, 0:T], AF.Silu, [r_lnt[i2], r_const], [r_zT[c]],
                  scale=colp[:, CP_LNG + c:CP_LNG + c + 1], bias=colp[:, CP_LNB + c:CP_LNB + c + 1])
        for s in range(nsub):
            P.tt("pool", h[:, s, :], h[:, s, :], rowp[:, 0, :], ALU.add, [r_h[s], r_const], [r_h[s]])
        proj_tokmajor(nsub, zT, r_zT, 22)
        rmsnorm_to_uT(nsub, CP_G_FFN1)
        ffn(nsub, 24, 32)
        for s in range(nsub):
            P.act(junk[:], h[:, s, :], AF.Square, [r_h[s]], [res("junk"), res("ssq")], accum_out=ssq[:, s:s + 1])
        P.act(rstd[:, 0:nsub], ssq[:, 0:nsub], AF.Ln, [res("ssq")], [res("rstd")], scale=1.0 / D, bias=RMS_EPS)
        P.act(rstd[:, 0:nsub], rstd[:, 0:nsub], AF.Exp, [res("rstd")], [res("rstd")], scale=-0.5)
        ov = ybuf[:].rearrange("p c t -> p (c t)").rearrange("p (s d) -> p s d", s=4)
        outs = []
        for s in range(nsub):
            P.stt(ov[:, s, :], h[:, s, :], rstd[:, s:s + 1], rowp[:, 1, :], ALU.mult, ALU.mult,
                  [r_h[s], res("rstd"), r_const], r_y)
            outs.append(P.dma("pool", outd[tok0 - 128 + s * 128: tok0 - 128 + (s + 1) * 128, :], ov[:, s, :],
                              reads=r_y))
        io.setdefault("_outs", []).extend(outs)


def make_phase_b_nc(tiles=None):
    nc = bass.Bass("TRN2", target_bir_lowering=False)
    if tiles is None:
        tiles = [(0, 1, False)] + [(128 + 512 * i, 4, True) for i in range(8)]
    io = {
        "xB": nc.dram_tensor("xB", [NTB, D], F32, kind="ExternalInput").ap(),
        "oT": nc.dram_tensor("oT", [128, 8, NTB], BF16, kind="ExternalInput").ap(),
        "w32": nc.dram_tensor("w32", [NWT_B, 128, 4096], F32, kind="ExternalInput").ap(),
        "w16": nc.dram_tensor("w16", [NWT_B, 128, 4096], BF16, kind="Internal").ap(),
        "wdg": nc.dram_tensor("wdg", [8, 128, 4096], BF16, kind="Internal").ap(),
        "colp": nc.dram_tensor("colp", [128, NCOLP], F32, kind="ExternalInput").ap(),
        "rowp": nc.dram_tensor("rowp", [128, 2, D], F32, kind="ExternalInput").ap(),
        "ident": nc.dram_tensor("ident", [128, 128], F32, kind="ExternalInput").ap(),
        "out": nc.dram_tensor("out", [NTB - 128, D], F32, kind="ExternalOutput").ap(),
    }
    P = Prog(nc)
    with contextlib.ExitStack() as es:
        C = Ctx(nc, es)
        build_phase_b(nc, P, C, io, tiles)
        fin = P.op("sp", lambda e: e.nop(), (), ())
        fin.deps.extend(io["_outs"])
        P.finish(es)
    return nc, P


WA_COLS = 1792
CA_G = 0
CA_LOG = 8
CA_NG = 14
NCOLA = 16
KA_IDENT, KA_NEGT, KA_NEGU, KA_MASKD, KA_MASKH, KA_ONES, KA_ZERO = range(7)
NCA = 7


def build_phase_a(nc, P, C, io, ntiles, do_attn=True, do_hg=True, out_fn=None, tile_hook=None, out_res=None):
    xA, wA32, colp_d, cst_d, mres_d = (io[k] for k in ("xA", "wA32", "colpA", "constA", "mres"))
    if out_fn is None:
        oTA = io["oTA"]
        out_fn = lambda q, j: oTA[:, q, j * 512:(j + 1) * 512]
    out_w = [] if out_res is None else [out_res]
    TM = 512
    NT = SEQ // TM
    wA = C.sb("wA", [128, 8, WA_COLS], BF16)
    KT = C.sb("KT", [128, 2, SEQ], BF16)
    V = C.sb("V", [128, SEQ // 128, 256], BF16)
    xt = C.sb("xt", [128, 4, D], F32)
    utm = C.sb("utm", [128, 4, D], BF16)
    uT = C.sb("uT", [128, 8, TM], BF16)
    QT = C.sb("QT", [128, 2, TM], BF16)
    hq = C.sb("hq", [128, 2, TM], F32)
    hsgs = C.sb("hsgs", [128, 2, TM], F32)
    hgt = C.sb("hgt", [128, 2, TM], F32)
    hi = C.sb("hi", [128, 4, 256], BF16)
    junk = C.sb("junk", [128, D], BF16)
    ssq = C.sb("ssq", [128, 4], F32)
    rstd = C.sb("rstd", [128, 4], F32)
    colp = C.sb("colpA", [128, NCOLA], F32)
    cst = C.sb("cst", [128, NCA, 128], BF16)
    mres = C.sb("mres", [128, TM], F32)
    lbt = C.sb("lbt", [128, 2, 4], F32)
    lb = C.sb("lb", [128, 2], F32)
    oml = C.sb("oml", [128, 2], F32)
    eZ = [C.sb(f"eZ{i}", [128, 2, TM], F32) for i in range(3)]
    spb = [C.sb(f"spb{i}", [128, 2, TM], BF16) for i in range(2)]
    Xb = [C.sb(f"Xb{i}", [128, 2, TM], BF16) for i in range(2)]
    Wb = [C.sb(f"Wb{i}", [128, 2, TM], BF16) for i in range(2)]
    osb = [C.sb(f"osb{i}", [128, TM], BF16) for i in range(2)]
    hff = C.sb("hff", [128, TM], F32)
    hg_ = C.sb("hg_", [128, TM], F32)
    hkk = C.sb("hkk", [128, TM], F32)
    hG = C.sb("hG", [128, TM], F32)
    heG = C.sb("heG", [128, TM], F32)
    hks = C.sb("hks", [128, TM], BF16)
    hqd = C.sb("hqd", [128, TM], BF16)
    hkib = C.sb("hkib", [128, TM], BF16)
    kstm = C.sb("kstm", [128, 4, 128], BF16)
    smk = [C.sb(f"smk{i}", [128, 128], BF16) for i in range(2)]
    Sst = [[C.sb(f"S{hd}_{i}", [128, 128], F32) for i in range(2)] for hd in range(2)]
    Sbf = [[C.sb(f"Sb{hd}_{i}", [128, 128], BF16) for i in range(2)] for hd in range(2)]
    osq = C.sb("osq", [128, TM], BF16)
    hrs = C.sb("hrs", [128, TM], F32)
    gsig = C.sb("gsig", [128, 2, TM], F32)
    ho1 = C.sb("ho1", [128, TM], F32)
    ohT = [C.sb(f"ohT{i}", [128, TM], BF16) for i in range(2)]
    pz0 = C.es.enter_context(nc.psum_tensor("p_z0", [128, 2, 512], F32))
    pz1 = C.es.enter_context(nc.psum_tensor("p_z1", [128, 2, 512], F32))
    pc = C.es.enter_context(nc.psum_tensor("p_c", [128, 2, 512], F32))
    psing = [C.ps(f"pa{i}") for i in range(6, 8)]
    pzz = (pz0, pz1)

    def pbk(b):
        if b < 2:
            return pz0[:, b, :]
        if b < 4:
            return pz1[:, b - 2, :]
        if b < 6:
            return pc[:, b - 4, :]
        return psing[b - 6][:]

    R = {}

    def res(n):
        if n not in R:
            R[n] = Res(n)
        return R[n]

    r_pb = [R.setdefault(f"pb{i}", Res(f"pb{i}", True)) for i in range(8)]
    r_const = res("const")
    r_xt = [res(f"xt{s}") for s in range(4)]
    r_utm = [res(f"utm{s}") for s in range(4)]
    r_uT = [res(f"uT{c}") for c in range(8)]
    r_KT = [[res(f"KT{p}_{j}") for j in range(NT)] for p in range(2)]
    r_V = [res(f"V{j}") for j in range(NT)]

    ident = cst[:, KA_IDENT, :]
    negT = cst[:, KA_NEGT, :]
    negU = cst[:, KA_NEGU, :]
    maskD = cst[:, KA_MASKD, :]
    maskH = cst[:, KA_MASKH, :]
    ones128 = cst[:, KA_ONES, :]
    zeroM = cst[:, KA_ZERO, :]

    P.dma("sp", colp[:], colp_d[:, :], writes=[r_const])
    P.dma("sp", mres[:], mres_d[:, :], writes=[r_const])
    P.dma("pool", cst[:], cst_d[:, :, :], writes=[r_const])
    for k in range(8):
        P.dma("pool", wA[:, k, :], wA32[:, k, :], writes=[res("wA")])
    lg = colp[:, CA_LOG:CA_LOG + 6].rearrange("p (h r) -> p h r", h=2)
    P.act(lbt[:, :, 0:3], lg, AF.Exp, [r_const], [res("lbt")])
    for hd in range(2):
        P.tt("dve", lbt[:, hd, 3:4], lbt[:, hd, 0:1], lbt[:, hd, 1:2], ALU.add, [res("lbt")], [res("lbt")])
        P.tt("dve", lbt[:, hd, 3:4], lbt[:, hd, 3:4], lbt[:, hd, 2:3], ALU.add, [res("lbt")], [res("lbt")])
        P.op("dve", (lambda hd: lambda e: e.reciprocal(out=lbt[:, hd, 3:4], in_=lbt[:, hd, 3:4]))(hd),
             [res("lbt")], [res("lbt")])
        P.tt("dve", lb[:, hd:hd + 1], lbt[:, hd, 0:1], lbt[:, hd, 3:4], ALU.mult, [res("lbt")], [res("lb")])
    P.ts("dve", oml[:], lb[:], -1.0, 1.0, ALU.mult, ALU.add, [res("lb")], [res("lb2")])
    for hd in range(2):
        P.memset("pool", Sst[hd][0][:], 0.0, [res(f"S{hd}_0")])
        P.memset("pool", Sbf[hd][0][:], 0.0, [res(f"Sb{hd}_0")])
    hstep = [0, 0]

    outs = io.setdefault("_outs", [])
    io["_zeroM"] = zeroM
    for j in range(ntiles):
        if tile_hook is not None:
            tile_hook(j)
        for s in range(4):
            P.dma("sp", xt[:, s, :], xA[j * TM + s * 128: j * TM + (s + 1) * 128, :], writes=[r_xt[s]])
        for s in range(4):
            P.act(junk[:], xt[:, s, :], AF.Square, [r_xt[s]], [res("junk"), res("ssq")], accum_out=ssq[:, s:s + 1])
        P.act(rstd[:], ssq[:], AF.Ln, [res("ssq")], [res("rstd")], scale=1.0 / D, bias=RMS_EPS)
        P.act(rstd[:], rstd[:], AF.Exp, [res("rstd")], [res("rstd")], scale=-0.5)
        for s in range(4):
            P.ts("dve", utm[:, s, :], xt[:, s, :], rstd[:, s:s + 1], None, ALU.mult, None,
                 [r_xt[s], res("rstd")], [r_utm[s]])
        for c in range(8):
            bank = 6 + (c % 2)
            pT = pbk(bank).bitcast(BF16)
            for s in range(4):
                P.tr(pT[:, s * 128:(s + 1) * 128], utm[:, s, c * 128:(c + 1) * 128], ident,
                     [r_utm[s], r_const], [r_pb[bank]])
            if c % 2 == 0:
                P.ts("dve", uT[:, c, :], pT[:, 0:TM], colp[:, CA_G + c:CA_G + c + 1], None, ALU.mult, None,
                     [r_pb[bank], r_const], [r_uT[c]])
            else:
                P.act(uT[:, c, :], pT[:, 0:TM], AF.Copy, [r_pb[bank], r_const], [r_uT[c]],
                      scale=colp[:, CA_G + c:CA_G + c + 1])
        for fc in range(10):
            bank = 4 + (fc % 2)
            for k in range(8):
                P.mm(pbk(bank), wA[:, k, fc * 128:(fc + 1) * 128], uT[:, k, :], k == 0, k == 7,
                     [res("wA"), r_uT[k]], [r_pb[bank]])
            if fc < 2:
                P.ts("dve", QT[:, fc, :], pbk(bank), 0.125, None, ALU.mult, None, [r_pb[bank]], [res(f"QT{fc}")])
            elif fc < 4:
                P.copy("dve", KT[:, fc - 2, j * TM:(j + 1) * TM], pbk(bank), [r_pb[bank]], [r_KT[fc - 2][j]])
            elif fc < 6:
                P.copy("dve", hq[:, fc - 4, :], pbk(bank), [r_pb[bank]], [res(f"hq{fc - 4}")])
            elif fc < 8:
                P.act(hsgs[:, fc - 6, :], pbk(bank), AF.Sigmoid, [r_pb[bank]], [res(f"hsg{fc - 6}")])
            else:
                P.copy("dve", hgt[:, fc - 8, :], pbk(bank), [r_pb[bank]], [res(f"hgt{fc - 8}")])
                P.act(gsig[:, fc - 8, :], pbk(bank), AF.Sigmoid, [r_pb[bank]], [res(f"gsig{fc - 8}")])
        for s in range(4):
            bank = 4 + s % 2
            for k in range(8):
                P.mm(pbk(bank), uT[:, k, s * 128:(s + 1) * 128], wA[:, k, 1280:1792], k == 0, k == 7,
                     [res("wA"), r_uT[k]], [r_pb[bank]])
            P.copy("dve", V[:, 4 * j + s, :], pbk(bank)[:, 0:256], [r_pb[bank]], [r_V[j]])
            P.act(hi[:, s, :], pbk(bank)[:, 256:512], AF.Copy, [r_pb[bank]], [res(f"hi{s}")])

        def hgrn_thunks(hd):
            rt = lambda n: res(f"hg_{n}")
            hsg = hsgs[:, hd, :]
            r_sg = res(f"hsg{hd}")
            gsl = gsig[:, hd, :]
            r_gs = res(f"gsig{hd}")
            OB = 3
            SB_ = 2
            th = []

            def p1a():
                P.ts("dve", hff[:], hsg, oml[:, hd:hd + 1], lb[:, hd:hd + 1], ALU.mult, ALU.add,
                     [r_sg, res("lb"), res("lb2")], [rt("f")])
                P.stt(gsl, hgt[:, hd, :], colp[:, CA_NG + hd:CA_NG + hd + 1], gsl, ALU.mult, ALU.mult,
                      [res(f"hgt{hd}"), r_gs, r_const], [r_gs])

            def p1b():
                P.act(hg_[:], hff[:], AF.Ln, [rt("f")], [rt("g")])
                P.ts("pool", hkk[:], hff[:], -1.0, 1.0, ALU.mult, ALU.add, [rt("f")], [rt("kk")])

            def p1c():
                P.op("dve", lambda e: e.tensor_tensor_scan(out=hG[:], data0=mres[:], data1=hg_[:], initial=0.0,
                                                           op0=ALU.mult, op1=ALU.add),
                     [rt("g"), r_const], [rt("G")])

            def p2a():
                P.act(heG[:], hG[:], AF.Exp, [rt("G")], [rt("eG")])
                P.act(hsg, hG[:], AF.Exp, [rt("G"), rt("f")], [r_sg], scale=-1.0)

            def p2b():
                P.tt("dve", hqd[:], hq[:, hd, :], heG[:], ALU.mult, [res(f"hq{hd}"), rt("eG")], [rt("qd")])
                P.tt("pool", hg_[:], hkk[:], hsg, ALU.mult, [rt("kk"), r_sg, rt("G")], [rt("g")])

            def p2c():
                P.copy("pool", hkib[:], hg_[:], [rt("g")], [rt("kib")])
                for ch in range(16):
                    P.ts("pool", hks[:, ch * 32:(ch + 1) * 32], hg_[:, ch * 32:(ch + 1) * 32],
                         heG[:, ch * 32 + 31:ch * 32 + 32], None, ALU.mult, None, [rt("g"), rt("eG")], [rt("ks")])

            def p3a():
                pT = pbk(7).bitcast(BF16)
                for s in range(4):
                    P.tr(pT[:, s * 128:(s + 1) * 128], hks[:, s * 128:(s + 1) * 128], ident, [rt("ks"), r_const],
                         [r_pb[7]])

            def p3b():
                pT = pbk(7).bitcast(BF16)
                P.copy("dve", kstm[:].rearrange("p s d -> p (s d)"), pT[:, 0:TM], [r_pb[7]], [rt("kstm")])
            th.extend([p1a, p1b, p1c, p2a, p2b, p2c, p3a, p3b])

            def bh_a(s):
                tsl = slice(s * 128, (s + 1) * 128)
                P.mm(pbk(SB_)[:, 0:128], hkib[:, tsl], hqd[:, tsl], True, True, [rt("kib"), rt("qd")], [r_pb[SB_]])

            def bh_b(s):
                P.tt("dve", smk[s % 2][:], pbk(SB_)[:, 0:128], maskH, ALU.mult, [r_pb[SB_], r_const],
                     [rt(f"smk{s % 2}")])

            def bh_c(s):
                tsl = slice(s * 128, (s + 1) * 128)
                P.mm(pbk(OB)[:, tsl], hi[:, s, hd * 128:(hd + 1) * 128], smk[s % 2][:], True, False,
                     [res(f"hi{s}"), rt(f"smk{s % 2}")], [r_pb[OB]])

            def cs_a(s, cc):
                n = hstep[hd]
                cur = n % 2
                csl = slice(s * 128 + cc * 32, s * 128 + (cc + 1) * 32)
                P.mm(pbk(OB)[:, csl], Sbf[hd][cur][:], hqd[:, csl], False, cc == 3,
                     [res(f"Sb{hd}_{cur}"), rt("qd")], [r_pb[OB]])
                psl = slice(cc * 32, (cc + 1) * 32)
                P.op("pe", lambda e: e.matmul(
                    pbk(SB_)[:, 128:256], lhsT=kstm[psl, s, :], rhs=hi[psl, s, hd * 128:(hd + 1) * 128],
                    start=True, stop=True, tile_position=(cc * 32, 0)),
                     [rt("kstm"), res(f"hi{s}")], [r_pb[SB_]])

            def cs_b(s, cc):
                n = hstep[hd]
                cur, nxt = n % 2, (n + 1) % 2
                ch = s * 4 + cc
                P.stt(Sst[hd][nxt][:], Sst[hd][cur][:], heG[:, ch * 32 + 31:ch * 32 + 32], pbk(SB_)[:, 128:256],
                      ALU.mult, ALU.add, [res(f"S{hd}_{cur}"), rt("eG"), r_pb[SB_]], [res(f"S{hd}_{nxt}")])
                P.copy("pool", Sbf[hd][nxt][:], Sst[hd][nxt][:], [res(f"S{hd}_{nxt}")], [res(f"Sb{hd}_{nxt}")])
                hstep[hd] += 1

            for s in range(4):
                th.append((lambda s: lambda: bh_a(s))(s))
                th.append((lambda s: lambda: bh_b(s))(s))
                th.append((lambda s: lambda: bh_c(s))(s))
                for cc in range(4):
                    th.append((lambda s, cc: lambda: cs_a(s, cc))(s, cc))
                    th.append((lambda s, cc: lambda: cs_b(s, cc))(s, cc))

            def f1a():
                P.act(osq[:], pbk(OB), AF.Square, [r_pb[OB]], [rt("osq")])

            def f1b():
                P.mm(pbk(SB_), ones128, osq[:], True, True, [r_const, rt("osq")], [r_pb[SB_]])

            def f1c():
                P.act(hrs[:], pbk(SB_), AF.Ln, [r_pb[SB_]], [rt("rs")], bias=RMS_EPS)
                P.act(hrs[:], hrs[:], AF.Exp, [rt("rs")], [rt("rs")], scale=-0.5)

            def f2a():
                P.tt("dve", ho1[:], pbk(OB), hrs[:], ALU.mult, [r_pb[OB], rt("rs")], [rt("o1")])

            def f2b():
                P.tt("pool", ohT[hd][:], ho1[:], gsl, ALU.mult, [rt("o1"), r_gs], [res(f"ohT{hd}")])
                outs.append(P.dma("pool", out_fn(2 + hd, j), ohT[hd][:], reads=[res(f"ohT{hd}")], writes=out_w))
            th.extend([f1a, f1b, f1c, f2a, f2b])
            return th

        if do_hg and not do_attn:
            for hd in range(2):
                for f_ in hgrn_thunks(hd):
                    f_()

        if do_attn:
            for p in range(2):
                cb = (4, 5)
                ob = 6
                for hh in range(2):
                    P.mm(pbk(cb[hh]), zeroM, uT[:, 0, :], True, False, [r_const, r_uT[0]], [r_pb[cb[hh]]])
                P.mm(pbk(ob), zeroM, uT[:, 0, :], True, False, [r_const, r_uT[0]], [r_pb[ob]])
                nkb = 4 * j + 4
                kbs = list(range(nkb - 1, -1, -1))
                extra = hgrn_thunks(p) if do_hg else []
                per_it = (len(extra) + nkb - 1) // nkb

                def q0_of(kb):
                    m = kb - 4 * j
                    return 128 * m if m > 0 else 0

                def st_qk(t):
                    kb = kbs[t]
                    q0 = q0_of(kb)
                    zi = 0
                    ksl = slice(kb * 128, (kb + 1) * 128)
                    for hh in range(2):
                        prt = slice(hh * 64, (hh + 1) * 64)
                        P.mm(pbk(2 * zi + hh)[:, q0:TM], KT[prt, p, ksl], QT[prt, p, q0:TM], True, True,
                             [r_KT[p][kb // 4], res(f"QT{p}")], [r_pb[2 * zi + hh]])

                def st_act(t):
                    kb = kbs[t]
                    q0 = q0_of(kb)
                    zi, e3, i2 = 0, t % 3, t % 2
                    P.act(eZ[e3][:, :, q0:TM], pzz[zi][:, :, q0:TM], AF.Exp, [r_pb[2 * zi], r_pb[2 * zi + 1]],
                          [res(f"eZ{e3}")])
                    if kb - 4 * j >= 0:
                        for hh in range(2):
                            P.tt("pool", eZ[e3][:, hh, q0:q0 + 128], eZ[e3][:, hh, q0:q0 + 128], maskD, ALU.mult,
                                 [res(f"eZ{e3}"), r_const], [res(f"eZ{e3}")])
                    P.act(spb[i2][:, :, q0:TM], eZ[e3][:, :, q0:TM], AF.Ln, [res(f"eZ{e3}")], [res(f"sp{i2}")],
                          bias=1.0)

                def st_negT(t):
                    q0 = q0_of(kbs[t])
                    i2 = t % 2
                    for hh in range(2):
                        P.mm(pbk(cb[hh])[:, q0:TM], negT, spb[i2][:, hh, q0:TM], False, False,
                             [r_const, res(f"sp{i2}")], [r_pb[cb[hh]]])

                def st_expc(t):
                    q0 = q0_of(kbs[t])
                    i2, e3 = t % 2, t % 3
                    P.act(Xb[i2][:, :, q0:TM], pc[:, :, q0:TM], AF.Exp, [r_pb[4], r_pb[5]], [res(f"X{i2}")])
                    P.tt("dve", Wb[i2][:, :, q0:TM], eZ[e3][:, :, q0:TM], Xb[i2][:, :, q0:TM], ALU.mult,
                         [res(f"eZ{e3}"), res(f"X{i2}")], [res(f"W{i2}")])

                def st_negU(t):
                    q0 = q0_of(kbs[t])
                    i2 = t % 2
                    for hh in range(2):
                        P.mm(pbk(cb[hh])[:, q0:TM], negU, spb[i2][:, hh, q0:TM], False, False,
                             [r_const, res(f"sp{i2}")], [r_pb[cb[hh]]])

                def st_pv(t):
                    kb = kbs[t]
                    q0 = q0_of(kb)
                    i2 = t % 2
                    for hh in range(2):
                        P.mm(pbk(ob)[hh * 64:(hh + 1) * 64, q0:TM],
                             V[:, kb, p * 128 + hh * 64:p * 128 + (hh + 1) * 64],
                             Wb[i2][:, hh, q0:TM], False, kb == 0, [r_V[kb // 4], res(f"W{i2}")], [r_pb[ob]])

                n = nkb
                st_qk(0)
                st_act(0)
                if n > 1:
                    st_qk(1)
                    st_act(1)
                st_negT(0)
                for t in range(n):
                    if t + 2 < n:
                        st_qk(t + 2)
                    st_expc(t)
                    if t + 1 < n:
                        st_negU(t)
                        st_negT(t + 1)
                    if t >= 1:
                        st_pv(t - 1)
                    if t + 2 < n:
                        st_act(t + 2)
                    for _ in range(per_it):
                        if extra:
                            extra.pop(0)()
                st_pv(n - 1)
                while extra:
                    extra.pop(0)()
                P.act(osb[p][:], pbk(ob), AF.Copy, [r_pb[ob]], [res(f"osb{p}")])
                outs.append(P.dma("pool", out_fn(p, j), osb[p][:], reads=[res(f"osb{p}")], writes=out_w))


def make_phase_a_nc(ntiles=16, do_attn=True, do_hg=True):
    nc = bass.Bass("TRN2", target_bir_lowering=False)
    io = {
        "xA": nc.dram_tensor("xA", [SEQ, D], F32, kind="ExternalInput").ap(),
        "wA32": nc.dram_tensor("wA32", [128, 8, WA_COLS], F32, kind="ExternalInput").ap(),
        "colpA": nc.dram_tensor("colpA", [128, NCOLA], F32, kind="ExternalInput").ap(),
        "constA": nc.dram_tensor("constA", [128, NCA, 128], F32, kind="ExternalInput").ap(),
        "mres": nc.dram_tensor("mres", [128, 512], F32, kind="ExternalInput").ap(),
        "oTA": nc.dram_tensor("oTA", [128, 4, SEQ], BF16, kind="ExternalOutput").ap(),
    }
    P = Prog(nc)
    with contextlib.ExitStack() as es:
        C = Ctx(nc, es)
        build_phase_a(nc, P, C, io, ntiles, do_attn, do_hg)
        fin = P.op("sp", lambda e: e.nop(), (), ())
        fin.deps.extend(io["_outs"])
        P.finish(es)
    return nc, P, C


def phase_a_inputs(inp, b, g):
    w = inp["w_in_ab"][0]
    cols = []
    for base in (0, 512):
        cols.append(w[:, base + 256 * g: base + 256 * g + 256])
    for base in (1536, 2048, 3072):
        cols.append(w[:, base + 256 * g: base + 256 * g + 256])
    cols.append(w[:, 1024 + 256 * g: 1024 + 256 * g + 256])
    cols.append(w[:, 2560 + 256 * g: 2560 + 256 * g + 256])
    wsel = np.concatenate(cols, axis=1)
    wA32 = np.ascontiguousarray(wsel.reshape(8, 128, WA_COLS).transpose(1, 0, 2))
    colp = np.zeros((128, NCOLA), np.float32)
    colp[:, CA_G:CA_G + 8] = colvec(inp["norm_mix_g"][0])
    lg = inp["hg_lb_logits"]
    for hd in range(2):
        head = 2 * g + hd
        colp[:, CA_LOG + hd * 3: CA_LOG + hd * 3 + 3] = lg[:, head * 128:(head + 1) * 128].T
        colp[:, CA_NG + hd] = inp["hg_norm_g"][0][head]
    return {"xA": np.ascontiguousarray(inp["x"][b]), "wA32": wA32, "colpA": colp}


def phase_a_consts():
    cst = np.zeros((128, NCA, 128), np.float32)
    jj = np.arange(128)[:, None]
    ss = np.arange(128)[None, :]
    cst[:, KA_IDENT, :] = np.eye(128)
    cst[:, KA_NEGT, :] = -1.0 * (jj >= ss)
    cst[:, KA_NEGU, :] = -1.0 * (jj < ss)
    cst[:, KA_MASKD, :] = (jj < ss)
    cst[:, KA_MASKH, :] = (jj <= ss) & ((jj // 32) == (ss // 32))
    cst[:, KA_ONES, :] = 1.0 / 128.0
    mres = np.ones((128, 512), np.float32)
    mres[:, ::32] = 0.0
    return cst, mres


GCOLS = 128 + SEQ
PAIRS = [[0, 1], [2, 3], [4, 5], [6, 7]]


def make_fused_nc(ntiles_a=16, tiles_b=None):
    nc = bass.Bass("TRN2", target_bir_lowering=False)
    if tiles_b is None:
        tiles_b = [(0, 1, False)] + [(128 + 512 * i, 4, True) for i in range(8)]
    io = {
        "xA": nc.dram_tensor("xA", [SEQ, D], F32, kind="ExternalInput").ap(),
        "wA32": nc.dram_tensor("wA32", [128, 8, WA_COLS], F32, kind="ExternalInput").ap(),
        "colpA": nc.dram_tensor("colpA", [128, NCOLA], F32, kind="ExternalInput").ap(),
        "constA": nc.dram_tensor("constA", [128, NCA, 128], F32, kind="ExternalInput").ap(),
        "mres": nc.dram_tensor("mres", [128, 512], F32, kind="ExternalInput").ap(),
        "xB": nc.dram_tensor("xB", [NTB, D], F32, kind="ExternalInput").ap(),
        "w32": nc.dram_tensor("w32", [NWT_B, 128, 4096], F32, kind="ExternalInput").ap(),
        "w16": nc.dram_tensor("w16", [NWT_B, 128, 4096], BF16).ap(),
        "wdg": nc.dram_tensor("wdg", [8, 128, 4096], BF16).ap(),
        "colp": nc.dram_tensor("colp", [128, NCOLP], F32, kind="ExternalInput").ap(),
        "rowp": nc.dram_tensor("rowp", [128, 2, D], F32, kind="ExternalInput").ap(),
        "ident": nc.dram_tensor("ident", [128, 128], F32, kind="ExternalInput").ap(),
        "out": nc.dram_tensor("out", [NTB - 128, D], F32, kind="ExternalOutput").ap(),
    }
    cc_src = [nc.dram_tensor(f"cc_src{j}", [512, 512], BF16) for j in range(16)]
    cc_dst = [nc.dram_tensor(f"cc_dst{j}", [1024, 512], BF16) for j in range(16)]
    cc_pad = nc.dram_tensor("cc_pad", [1024, 128], BF16)
    P = Prog(nc)
    with contextlib.ExitStack() as es0:
        P.alloc_sems(es0)
        r_dst = Res("cc_dst")
        pre = precast_ops(io)
        pre_res = {}
        per_tile = (len(pre) + ntiles_a - 1) // ntiles_a
        state = {"ccops": []}

        def gather_tile(j):
            deps = io["_outs"][-4:]
            op = P.cc((lambda j: lambda e: e.collective_compute(
                "AllGather", ALU.bypass, replica_groups=PAIRS,
                ins=[cc_src[j].ap().opt()], outs=[cc_dst[j].ap().opt()]))(j), [], [])
            op.deps.extend(deps)
            state["ccops"].append(op)

        def hook(j):
            if j > 0:
                gather_tile(j - 1)
            for a in pre[j * per_tile:(j + 1) * per_tile]:
                r = pre_res.setdefault(a[2], Res(a[2]))
                P.dma("pool", a[0], a[1], writes=[r])

        with contextlib.ExitStack() as esA:
            C = Ctx(nc, esA)
            build_phase_a(nc, P, C, io, ntiles_a,
                          out_fn=lambda q, j: cc_src[j].ap()[q * 128:(q + 1) * 128, :],
                          tile_hook=hook)
            gather_tile(ntiles_a - 1)
            zpad = [P.dma("pool", cc_pad.ap()[q * 128:(q + 1) * 128, :], io["_zeroM"]) for q in range(8)]
            io["_outs"] = []
            fin_cc = P.op("pool", lambda e: e.nop(), (), [r_dst])
            fin_cc.deps.extend(state["ccops"][-1:] + zpad)
            P.barrier()
            P.emit_segment()
            sbA = C.sb_bytes
        with contextlib.ExitStack() as esB:
            C = Ctx(nc, esB, "b_")

            def gsrc(half, tok0, T):
                t = 4096 * half - 128 + tok0
                if t < 0:
                    return cc_pad.ap().rearrange("(k p) t -> p k t", p=128)[:, :, 0:T]
                j, c0 = divmod(t, 512)
                return cc_dst[j].ap().rearrange("(k p) t -> p k t", p=128)[:, :, c0:c0 + T]

            build_phase_b(nc, P, C, io, tiles_b, precast_done=pre_res, gathered=gsrc, gathered_res=r_dst)
            fin = P.op("sp", lambda e: e.nop(), (), ())
            fin.deps.extend(io["_outs"])
            P.emit_segment()
            sbB = C.sb_bytes
    P.sb_bytes = (sbA, sbB)
    return nc, P


def wtile_kn(W, col0, ncols=512):
    return np.ascontiguousarray(W[:, col0:col0 + ncols].reshape(8, 128, ncols).transpose(1, 0, 2)).reshape(128, -1)


def phase_b_weights(inp, w_out_perm):
    tiles = []
    for half in range(2):
        tiles.append(wtile_kn(w_out_perm, half * 512))
    for layer in range(2):
        pass
    w1 = inp["w_ff1"]
    w2 = inp["w_ff2"]
    wg = inp["conv_w_glu"][0]
    wp = inp["conv_w_pw"][0]

    def ff1_tiles(l):
        return [wtile_kn(w1[l], j * 512) for j in range(8)]

    def ff2_tiles(l):
        out = []
        for half in range(2):
            for cg in range(4):
                blk = w2[l][cg * 1024:(cg + 1) * 1024, half * 512:(half + 1) * 512]
                out.append(np.ascontiguousarray(blk.reshape(8, 128, 512).transpose(1, 0, 2)).reshape(128, -1))
        return out

    tiles += ff1_tiles(0) + ff2_tiles(0)
    for j in range(4):
        cols = []
        for q in range(2):
            cols.append(wg[:, (2 * j + q) * 128:(2 * j + q + 1) * 128])
        for q in range(2):
            cols.append(wg[:, 1024 + (2 * j + q) * 128:1024 + (2 * j + q + 1) * 128])
        blk = np.concatenate(cols, axis=1)
        tiles.append(wtile_kn(blk, 0))
    for half in range(2):
        tiles.append(wtile_kn(wp, half * 512))
    tiles += ff1_tiles(1) + ff2_tiles(1)
    assert len(tiles) == NWT_B
    return np.stack(tiles).astype(np.float32)


def colvec(v):
    return np.ascontiguousarray(v.reshape(8, 128).T)


def phase_b_params(inp, halo_flag):
    colp = np.zeros((128, NCOLP), np.float32)
    colp[:, CP_G_FFN0:CP_G_FFN0 + 8] = colvec(inp["norm_ffn_g"][0])
    colp[:, CP_G_MIX1:CP_G_MIX1 + 8] = colvec(inp["norm_mix_g"][1])
    colp[:, CP_G_FFN1:CP_G_FFN1 + 8] = colvec(inp["norm_ffn_g"][1])
    colp[:, CP_BA:CP_BA + 8] = colvec(inp["conv_b_glu"][0][:1024])
    colp[:, CP_BB:CP_BB + 8] = colvec(inp["conv_b_glu"][0][1024:])
    colp[:, CP_BDW:CP_BDW + 8] = colvec(inp["conv_b_dw"][0])
    colp[:, CP_LNG:CP_LNG + 8] = colvec(inp["conv_ln_g"][0])
    colp[:, CP_LNB:CP_LNB + 8] = colvec(inp["conv_ln_b"][0])
    colp[:, CP_HALO] = halo_flag
    colp[:, CP_F0] = 1.0 - halo_flag
    colp[:, CP_F1] = halo_flag
    wdw = inp["conv_w_dw"][0]
    for c in range(8):
        colp[:, CP_WDW + c * CONVW: CP_WDW + (c + 1) * CONVW] = wdw[:, c * 128:(c + 1) * 128].T
    rowp = np.zeros((128, 2, D), np.float32)
    rowp[:, 0, :] = inp["conv_b_pw"][0][None, :]
    rowp[:, 1, :] = inp["final_norm_g"][None, :]
    return colp, rowp


def w_out_permuted(inp):
    w = inp["w_out_ab"][0]
    rows = []
    for gg in range(2):
        rows.append(w[256 * gg: 256 * gg + 256])
        rows.append(w[512 + 256 * gg: 512 + 256 * gg + 256])
    return np.concatenate(rows, axis=0)


def kernel(**inputs):
    inp = {k: np.asarray(v) for k, v in inputs.items()}
    cores = list(range(NCORES))
    nc, _ = make_fused_nc()
    cst, mres = phase_a_consts()
    w32 = phase_b_weights(inp, w_out_permuted(inp))
    ident = np.eye(128, dtype=np.float32)
    maps = []
    for c in cores:
        b, g = divmod(c, 2)
        m = phase_a_inputs(inp, b, g)
        m["constA"] = cst
        m["mres"] = mres
        colp, rowp = phase_b_params(inp, float(g))
        xB = np.zeros((NTB, D), np.float32)
        t0 = 4096 * g - 128
        lo = max(t0, 0)
        xB[lo - t0:] = inp["x"][b, lo:t0 + NTB]
        m.update({"xB": xB, "w32": w32, "colp": colp, "rowp": rowp, "ident": ident})
        maps.append(m)
    res = run_bass_kernel_spmd(nc, maps, core_ids=cores)
    out = np.zeros((BATCH, SEQ, D), np.float32)
    for c in cores:
        b, g = divmod(c, 2)
        out[b, 4096 * g:4096 * (g + 1)] = np.asarray(res.results[c]["out"])
    return out
```

```python
import contextlib
import numpy as np
import ml_dtypes
import concourse.bass as bass
import concourse.mybir as mybir
from concourse.bass_utils import run_bass_kernel_spmd

F32 = mybir.dt.float32
BF16 = mybir.dt.bfloat16
F32R = mybir.dt.float32r
AF = mybir.ActivationFunctionType
ALU = mybir.AluOpType

D = 1024
DFF = 4096
SEQ = 8192
BATCH = 4
NCORES = 8
RMS_EPS = 1e-6
LN_EPS = 1e-5
CONVW = 31


class Res:
    __slots__ = ("name", "last_w", "readers", "excl")

    def __init__(self, name, excl=False):
        self.name = name
        self.last_w = None
        self.readers = []
        self.excl = excl


class Op:
    __slots__ = ("eng", "fn", "deps", "sig", "sigcount", "dma", "dsem", "dval", "prewait", "inc")

    def __init__(self, eng, fn, dma):
        self.eng = eng
        self.fn = fn
        self.deps = []
        self.sig = False
        self.sigcount = 0
        self.dma = dma
        self.dsem = None
        self.dval = 0
        self.prewait = None
        self.inc = 16


class Prog:
    ENGS = ("pe", "act", "dve", "pool", "sp")
    DMA_ENGS = ("sp", "pool", "act")

    def __init__(self, nc, ndma_sems=8):
        self.nc = nc
        self.ops = {e: [] for e in self.ENGS}
        self.ndma = {e: 0 for e in self.ENGS}
        self.K = ndma_sems
        self.all_ops = []

    def _add(self, eng, fn, reads, writes, dma):
        op = Op(eng, fn, dma)
        deps = []
        for r in reads:
            if r.last_w is not None:
                deps.append(r.last_w)
            if r.excl:
                deps.extend(x for x in r.readers if x.eng != eng)
        for w in writes:
            if w.last_w is not None:
                deps.append(w.last_w)
            deps.extend(w.readers)
        for r in reads:
            r.readers.append(op)
        for w in writes:
            w.last_w = op
            w.readers = []
        seen = set()
        for d in deps:
            if id(d) not in seen and d is not op:
                seen.add(id(d))
                op.deps.append(d)
        if dma:
            i = self.ndma[eng]
            self.ndma[eng] += 1
            op.dsem = (eng, i % self.K)
            op.dval = 16 * (i // self.K + 1)
            if i >= self.K:
                op.prewait = (op.dsem, 16 * (i // self.K))
        self.ops[eng].append(op)
        self.all_ops.append(op)
        return op

    def op(self, eng, fn, reads=(), writes=()):
        return self._add(eng, fn, reads, writes, False)

    def dma(self, eng, out, in_, reads=(), writes=()):
        return self._add(eng, lambda e: e.dma_start(out=out, in_=in_), reads, writes, True)

    def mm(self, out, lhsT, rhs, start, stop, reads, writes):
        return self.op("pe", lambda e: e.matmul(out, lhsT=lhsT, rhs=rhs, start=start, stop=stop),
                       reads, writes)

    def tr(self, out, in_, ident, reads, writes):
        return self.op("pe", lambda e: e.transpose(out=out, in_=in_, identity=ident), reads, writes)

    def act(self, out, in_, func, reads, writes, **kw):
        return self.op("act", lambda e: e.activation(out=out, in_=in_, func=func, **kw), reads, writes)

    def ts(self, eng, out, in0, s1, s2, op0, op1, reads, writes):
        if op1 is None:
            return self.op(eng, lambda e: e.tensor_scalar(out=out, in0=in0, scalar1=s1, scalar2=None, op0=op0),
                           reads, writes)
        return self.op(eng, lambda e: e.tensor_scalar(out=out, in0=in0, scalar1=s1, scalar2=s2, op0=op0, op1=op1),
                       reads, writes)

    def tt(self, eng, out, in0, in1, op, reads, writes):
        return self.op(eng, lambda e: e.tensor_tensor(out=out, in0=in0, in1=in1, op=op), reads, writes)

    def stt(self, out, in0, scalar, in1, op0, op1, reads, writes):
        return self.op("dve", lambda e: e.scalar_tensor_tensor(out=out, in0=in0, scalar=scalar, in1=in1,
                                                               op0=op0, op1=op1), reads, writes)

    def copy(self, eng, out, in_, reads, writes):
        return self.op(eng, lambda e: e.tensor_copy(out=out, in_=in_), reads, writes)

    def memset(self, eng, out, val, writes):
        return self.op(eng, lambda e: e.memset(out, val), (), writes)

    def cc(self, fn, reads, writes):
        op = self._add("pool", fn, reads, writes, False)
        op.dma = True
        self.ncc = getattr(self, "ncc", 0) + 1
        op.dsem = ("cc", 0)
        op.dval = self.ncc
        op.inc = 1
        return op

    def barrier(self):
        tails = []
        for e in self.ENGS:
            comp = [o for o in self.ops[e] if not o.dma]
            if comp:
                tails.append(comp[-1])
            dm = [o for o in self.ops[e] if o.dma and o.dsem[0] != "cc"]
            tails.extend(dm[-self.K:])
        for e in self.ENGS:
            op = self.op(e, lambda eng: eng.nop(), (), ())
            op.deps.extend(t for t in tails if t is not op)

    def _skip(self, d, op):
        return (not d.dma) and d.eng == op.eng and op.eng == "pe" and not op.dma

    def alloc_sems(self, es):
        nc = self.nc
        self.sems = {"eng": {e: es.enter_context(nc.semaphore(f"se_{e}")) for e in self.ENGS},
                     "dma": {(e, i): es.enter_context(nc.semaphore(f"sd_{e}{i}"))
                             for e in self.DMA_ENGS for i in range(self.K)}}
        self.sems["dma"][("cc", 0)] = es.enter_context(nc.semaphore("s_cc"))
        self.emitted = {e: 0 for e in self.ENGS}
        self.sigc = {e: 0 for e in self.ENGS}
        self.waited = {e: {} for e in self.ENGS}
        self.nwaits = 0

    def plan(self):
        seg = {e: self.ops[e][self.emitted[e]:] for e in self.ENGS}
        segset = set(id(o) for e in self.ENGS for o in seg[e])
        for e in self.ENGS:
            for op in seg[e]:
                for d in op.deps:
                    if d.dma or self._skip(d, op):
                        continue
                    if id(d) not in segset:
                        assert d.sig, "cross-segment dependency on a non-signalling op"
                    d.sig = True
        for e in self.ENGS:
            for op in seg[e]:
                if op.sig and not op.dma:
                    self.sigc[e] += 1
                    op.sigcount = self.sigc[e]
        plans = {}
        for e in self.ENGS:
            waited = self.waited[e]
            plan = []
            for op in seg[e]:
                ws = {}
                if op.prewait is not None:
                    k, v = op.prewait
                    ws[("d",) + k] = v
                for d in op.deps:
                    if d.dma:
                        k = ("d",) + d.dsem
                        v = d.dval
                    else:
                        if self._skip(d, op):
                            continue
                        k = ("e", d.eng)
                        v = d.sigcount
                    if ws.get(k, 0) < v:
                        ws[k] = v
                wl = []
                for k, v in ws.items():
                    if waited.get(k, 0) >= v:
                        continue
                    waited[k] = v
                    wl.append((k, v))
                self.nwaits += len(wl)
                plan.append((op, wl))
            plans[e] = plan
            self.emitted[e] = len(self.ops[e])
        return plans

    def run_engine(self, e, engine, plans):
        esem = self.sems["eng"]
        dsem = self.sems["dma"]
        for op, wl in plans[e]:
            for k, v in wl:
                if k[0] == "d":
                    engine.wait_ge(dsem[(k[1], k[2])], v)
                else:
                    engine.wait_ge(esem[k[1]], v)
            ins = op.fn(engine)
            if op.dma:
                ins.then_inc(dsem[op.dsem], getattr(op, "inc", 16))
            elif op.sig:
                ins.then_inc(esem[e], 1)

    def emit_segment(self):
        plans = self.plan()
        P = self
        with self.nc.Block() as block:
            @block.sync
            def _(e):
                P.run_engine("sp", e, plans)

            @block.tensor
            def _(e):
                P.run_engine("pe", e, plans)

            @block.scalar
            def _(e):
                P.run_engine("act", e, plans)

            @block.vector
            def _(e):
                P.run_engine("dve", e, plans)

            @block.gpsimd
            def _(e):
                P.run_engine("pool", e, plans)

    def finish(self, es):
        self.alloc_sems(es)
        self.emit_segment()


class Ctx:
    def __init__(self, nc, es, prefix=""):
        self.nc = nc
        self.es = es
        self.sb_bytes = 0
        self.prefix = prefix

    def sb(self, name, shape, dt):
        n = 1
        for s in shape[1:]:
            n *= s
        self.sb_bytes += n * (4 if dt in (F32, F32R) else 2)
        return self.es.enter_context(self.nc.sbuf_tensor("s_" + self.prefix + name, shape, dt))

    def ps(self, name):
        return self.es.enter_context(self.nc.psum_tensor("p_" + self.prefix + name, [128, 512], F32))


NTB = 4096 + 128
NWT_B = 40
CP_G_FFN0 = 0
CP_G_MIX1 = 8
CP_G_FFN1 = 16
CP_BA = 24
CP_BB = 32
CP_BDW = 40
CP_LNG = 48
CP_LNB = 56
CP_HALO = 64
CP_F0 = 65
CP_F1 = 66
CP_WDW = 72
NCOLP = CP_WDW + 8 * CONVW


def precast_ops(io):
    w32, w16 = io["w32"], io["w16"]
    out = []
    for i in range(w32.shape[0]):
        for hh in range(2):
            out.append((w16[i, :, hh * 2048:(hh + 1) * 2048], w32[i, :, hh * 2048:(hh + 1) * 2048], f"w16_{i}_{hh}"))
    return out


def build_phase_b(nc, P, C, io, tiles, precast_done=None, gathered=None, gathered_res=None):
    xB, w32, w16, colp_d, rowp_d, ident_d, outd = (io[k] for k in
                                                  ("xB", "w32", "w16", "colp", "rowp", "ident", "out"))
    oTd = io.get("oT")
    NW = 5
    TM = 512
    h = C.sb("h", [128, 4, D], F32)
    utm = C.sb("utm", [128, 4, D], BF16)
    uT = C.sb("uT", [128, 8, TM], BF16)
    hidT = C.sb("hidT", [128, 32, TM], BF16)
    wb = [C.sb(f"wb{i}", [128, 4096], BF16) for i in range(NW)]
    oTt = C.sb("oTt", [128, 8, TM], BF16)
    if gathered is not None:
        oTa = C.sb("oTa", [128, 8, TM], BF16)
        oTb = C.sb("oTb", [128, 8, TM], BF16)
    aT = C.sb("aT", [128, 8, 32 + TM], BF16)
    ybuf = C.sb("ybuf", [128, 8, TM], F32)
    zT = C.sb("zT", [128, 8, TM], BF16)
    sig = [C.sb(f"sig{i}", [128, TM], F32) for i in range(2)]
    rtmp = [C.sb(f"rtmp{i}", [128, TM], BF16) for i in range(2)]
    junk = C.sb("junk", [128, D], BF16)
    ysq = C.sb("ysq", [128, TM], F32)
    lnm = C.sb("lnm", [128, TM], F32)
    lnm2 = C.sb("lnm2", [128, TM], F32)
    lnr = C.sb("lnr", [128, TM], F32)
    lnt = [C.sb(f"lnt{i}", [128, TM], F32) for i in range(2)]
    ssq = C.sb("ssq", [128, 4], F32)
    rstd = C.sb("rstd", [128, 4], F32)
    colp = C.sb("colp", [128, NCOLP], F32)
    rowp = C.sb("rowp", [128, 2, D], F32)
    ident = C.sb("ident", [128, 128], BF16)
    ones_r = C.sb("ones_r", [128, 128], F32)
    pb = [C.ps(f"pb{i}") for i in range(8)]

    R = {}

    def res(n):
        if n not in R:
            R[n] = Res(n)
        return R[n]

    r_h = [res(f"h{s}") for s in range(4)]
    r_utm = [res(f"utm{s}") for s in range(4)]
    r_uT = [res(f"uT{c}") for c in range(8)]
    r_hid = [res(f"hid{c}") for c in range(32)]
    r_wb = [res(f"wb{i}") for i in range(NW)]
    r_pb = [R.setdefault(f"pb{i}", Res(f"pb{i}", True)) for i in range(8)]
    r_aT = [res(f"aT{c}") for c in range(8)]
    r_y = [res(f"y{c}") for c in range(8)]
    r_zT = [res(f"zT{c}") for c in range(8)]
    r_sig = [res("sig0"), res("sig1")]
    r_rt = [res("rt0"), res("rt1")]
    r_lnt = [res("lnt0"), res("lnt1")]
    r_const = res("const")
    r_w16 = [res(f"w16_{i}") for i in range(NWT_B)]

    P.dma("sp", colp[:], colp_d[:, :], writes=[r_const])
    P.dma("sp", rowp[:], rowp_d[:, :, :], writes=[r_const])
    P.dma("pool", ident[:], ident_d[:, :], writes=[r_const])
    P.memset("pool", ones_r[:], 1.0 / D, writes=[r_const])
    for c in range(8):
        P.memset("pool", aT[:, c, 0:32], 0.0, writes=[r_aT[c]])

    if precast_done is None:
        for op_args in precast_ops(io):
            P.dma("pool", op_args[0], op_args[1], writes=[res(op_args[2])])
    else:
        for n, r in precast_done.items():
            R[n] = r

    wstate = {"n": 0}

    def wload(tile_idx, src=None, src_res=None):
        i = wstate["n"] % NW
        wstate["n"] += 1
        if src is None:
            P.dma("sp", wb[i][:], w16[tile_idx, :, :], reads=[res(f"w16_{tile_idx}_0"), res(f"w16_{tile_idx}_1")],
                  writes=[r_wb[i]])
        else:
            P.dma("sp", wb[i][:], src, reads=src_res, writes=[r_wb[i]])
        return wb[i], r_wb[i]

    wdg = io["wdg"]
    for c in range(8):
        cc4 = c % 4
        stg = hidT[:, 8 * cc4:8 * cc4 + 8, :].rearrange("p a b -> p (a b)")
        rs = [r_hid[8 * cc4 + q] for q in range(8)]
        for k in range(CONVW):
            wcol = CP_WDW + c * CONVW + k
            if (c + k) % 2 == 0:
                P.ts("dve", stg[:, k * 128:(k + 1) * 128], ident[:], colp[:, wcol:wcol + 1], None, ALU.mult, None,
                     [r_const] + rs, rs)
            else:
                P.act(stg[:, k * 128:(k + 1) * 128], ident[:], AF.Copy, [r_const] + rs, rs,
                      scale=colp[:, wcol:wcol + 1])
        P.dma("pool", wdg[c, :, 0:CONVW * 128], stg[:, 0:CONVW * 128], reads=rs, writes=[res(f"wdg{c}")])

    def rmsnorm_to_uT(nsub, gcol):
        T = nsub * 128
        for s in range(nsub):
            P.act(junk[:], h[:, s, :], AF.Square, [r_h[s]], [res("junk"), res("ssq")], accum_out=ssq[:, s:s + 1])
        P.act(rstd[:, 0:nsub], ssq[:, 0:nsub], AF.Ln, [res("ssq")], [res("rstd")], scale=1.0 / D, bias=RMS_EPS)
        P.act(rstd[:, 0:nsub], rstd[:, 0:nsub], AF.Exp, [res("rstd")], [res("rstd")], scale=-0.5)
        for s in range(nsub):
            P.ts("dve", utm[:, s, :], h[:, s, :], rstd[:, s:s + 1], None, ALU.mult, None,
                 [r_h[s], res("rstd")], [r_utm[s]])
        for c in range(8):
            bank = 6 + (c % 2)
            pT = pb[bank][:].bitcast(BF16)
            for s in range(nsub):
                P.tr(pT[:, s * 128:(s + 1) * 128], utm[:, s, c * 128:(c + 1) * 128], ident[:],
                     [r_utm[s], r_const], [r_pb[bank]])
            if c % 2 == 0:
                P.ts("dve", uT[:, c, 0:T], pT[:, 0:T], colp[:, gcol + c:gcol + c + 1], None, ALU.mult, None,
                     [r_pb[bank], r_const], [r_uT[c]])
            else:
                P.act(uT[:, c, 0:T], pT[:, 0:T], AF.Copy, [r_pb[bank], r_const], [r_uT[c]],
                      scale=colp[:, gcol + c:gcol + c + 1])

    def proj_tokmajor(nsub, srcT, r_src, wbase):
        for half in range(2):
            wt, rw = wload(wbase + half)
            wv = wt[:].rearrange("p (k n) -> p k n", k=8)
            for s in range(nsub):
                for k in range(8):
                    P.mm(pb[s][:, :], srcT[:, k, s * 128:(s + 1) * 128], wv[:, k, :], k == 0, k == 7,
                         [r_src[k], rw], [r_pb[s]])
                P.tt("dve", h[:, s, half * 512:(half + 1) * 512], pb[s][:, :], h[:, s, half * 512:(half + 1) * 512],
                     ALU.add, [r_pb[s], r_h[s]], [r_h[s]])

    def ffn(nsub, w1base, w2base):
        T = nsub * 128
        for j in range(8):
            wt, rw = wload(w1base + j)
            wv = wt[:].rearrange("p (k n) -> p k n", k=8)
            for cc in range(4):
                c = 4 * j + cc
                bank = 4 + (c % 2)
                for k in range(8):
                    P.mm(pb[bank][:, 0:T], wv[:, k, cc * 128:(cc + 1) * 128], uT[:, k, 0:T], k == 0, k == 7,
                         [rw, r_uT[k]], [r_pb[bank]])
                P.act(rtmp[c % 2][:, 0:T], pb[bank][:, 0:T], AF.Relu, [r_pb[bank]], [r_rt[c % 2]])
                P.tt("dve", hidT[:, c, 0:T], rtmp[c % 2][:, 0:T], pb[bank][:, 0:T], ALU.mult,
                     [r_rt[c % 2], r_pb[bank]], [r_hid[c]])
        for half in range(2):
            for cg in range(4):
                wt, rw = wload(w2base + half * 4 + cg)
                wv = wt[:].rearrange("p (k n) -> p k n", k=8)
                for s in range(nsub):
                    for cc in range(8):
                        c = cg * 8 + cc
                        P.mm(pb[s][:, :], hidT[:, c, s * 128:(s + 1) * 128], wv[:, cc, :], c == 0, c == 31,
                             [r_hid[c], rw], [r_pb[s]])
            for s in range(nsub):
                P.tt("dve", h[:, s, half * 512:(half + 1) * 512], pb[s][:, :], h[:, s, half * 512:(half + 1) * 512],
                     ALU.add, [r_pb[s], r_h[s]], [r_h[s]])

    prev_T = None
    for ti, (tok0, nsub, full) in enumerate(tiles):
        T = nsub * 128
        for s in range(nsub):
            P.dma("sp", h[:, s, :], xB[tok0 + s * 128: tok0 + (s + 1) * 128, :], writes=[r_h[s]])
        r_oTt = [res(f"oTt{k}") for k in range(8)]
        if gathered is None:
            P.dma("sp", oTt[:, :, 0:T], oTd[:, :, tok0:tok0 + T], writes=r_oTt)
        else:
            P.dma("sp", oTa[:, :, 0:T], gathered(0, tok0, T), reads=[gathered_res], writes=[res("oTa")])
            P.dma("sp", oTb[:, :, 0:T], gathered(1, tok0, T), reads=[gathered_res], writes=[res("oTb")])
            P.ts("pool", oTb[:, :, 0:T], oTb[:, :, 0:T], colp[:, CP_F1:CP_F1 + 1], None, ALU.mult, None,
                 [res("oTb"), r_const], [res("oTb")])
            P.stt(oTt[:, :, 0:T], oTa[:, :, 0:T], colp[:, CP_F0:CP_F0 + 1], oTb[:, :, 0:T], ALU.mult, ALU.add,
                  [res("oTa"), res("oTb"), r_const], r_oTt)
        proj_tokmajor(nsub, oTt, [res(f"oTt{k}") for k in range(8)], 0)
        rmsnorm_to_uT(nsub, CP_G_FFN0)
        ffn(nsub, 2, 10)
        rmsnorm_to_uT(nsub, CP_G_MIX1)
        if prev_T is not None:
            for c in range(8):
                if ti == 1:
                    P.ts("pool", aT[:, c, 2:32], aT[:, c, 2 + prev_T:32 + prev_T], colp[:, CP_HALO:CP_HALO + 1], None,
                         ALU.mult, None, [r_aT[c], r_const], [r_aT[c]])
                else:
                    P.copy("pool", aT[:, c, 2:32], aT[:, c, 2 + prev_T:32 + prev_T], [r_aT[c]], [r_aT[c]])
        for j in range(4):
            wt, rw = wload(18 + j)
            wv = wt[:].rearrange("p (k n) -> p k n", k=8)
            for q in range(2):
                c = 2 * j + q
                ba_, bb_ = (4, 5) if c % 2 == 0 else (6, 7)
                for k in range(8):
                    P.mm(pb[ba_][:, 0:T], wv[:, k, q * 128:(q + 1) * 128], uT[:, k, 0:T], k == 0, k == 7,
                         [rw, r_uT[k]], [r_pb[ba_]])
                for k in range(8):
                    P.mm(pb[bb_][:, 0:T], wv[:, k, (2 + q) * 128:(3 + q) * 128], uT[:, k, 0:T], k == 0, k == 7,
                         [rw, r_uT[k]], [r_pb[bb_]])
                P.act(sig[c % 2][:, 0:T], pb[bb_][:, 0:T], AF.Sigmoid, [r_pb[bb_], r_const], [r_sig[c % 2]],
                      bias=colp[:, CP_BB + c:CP_BB + c + 1])
                P.stt(aT[:, c, 32:32 + T], pb[ba_][:, 0:T], colp[:, CP_BA + c:CP_BA + c + 1], sig[c % 2][:, 0:T],
                      ALU.add, ALU.mult, [r_pb[ba_], r_sig[c % 2], r_const], [r_aT[c]])
        prev_T = T
        if not full:
            continue
        for c in range(8):
            wt, rw = wload(None, wdg[c, :, :], [res(f"wdg{c}")])
            bank = 4 + (c % 2)
            for k in range(CONVW):
                P.mm(pb[bank][:, 0:T], wt[:, k * 128:(k + 1) * 128], aT[:, c, 2 + k:2 + k + T], k == 0,
                     k == CONVW - 1, [rw, r_aT[c]], [r_pb[bank]])
            P.act(ybuf[:, c, 0:T], pb[bank][:, 0:T], AF.Identity, [r_pb[bank], r_const], [r_y[c]],
                  bias=colp[:, CP_BDW + c:CP_BDW + c + 1])
        for c in range(8):
            P.mm(pb[6][:, 0:T], ones_r[:], ybuf[:, c, 0:T], c == 0, c == 7,
                 [r_const, r_y[c]], [r_pb[6]])
        for c in range(8):
            P.act(ysq[:, 0:T], ybuf[:, c, 0:T], AF.Square, [r_y[c]], [res("ysq")])
            P.mm(pb[7][:, 0:T], ones_r[:], ysq[:, 0:T], c == 0, c == 7,
                 [r_const, res("ysq")], [r_pb[7]])
        P.act(lnm[:, 0:T], pb[6][:, 0:T], AF.Identity, [r_pb[6]], [res("lnm")])
        P.act(lnm2[:, 0:T], pb[6][:, 0:T], AF.Square, [r_pb[6]], [res("lnm2")])
        P.tt("dve", lnr[:, 0:T], pb[7][:, 0:T], lnm2[:, 0:T], ALU.subtract, [r_pb[7], res("lnm2")], [res("lnr")])
        P.act(lnr[:, 0:T], lnr[:, 0:T], AF.Ln, [res("lnr")], [res("lnr")], bias=LN_EPS)
        P.act(lnr[:, 0:T], lnr[:, 0:T], AF.Exp, [res("lnr")], [res("lnr")], scale=-0.5)
        for c in range(8):
            i2 = c % 2
            P.tt("pool", lnt[i2][:, 0:T], ybuf[:, c, 0:T], lnm[:, 0:T], ALU.subtract, [r_y[c], res("lnm")],
                 [r_lnt[i2]])
            P.tt("dve", lnt[i2][:, 0:T], lnt[i2][:, 0:T], lnr[:, 0:T], ALU.mult, [r_lnt[i2], res("lnr")],
                 [r_lnt[i2]])
            P.act(zT[:, c, 0:T], lnt[i2][:, 0:T], AF.Silu, [r_lnt[i2], r_const], [r_zT[c]],
                  scale=colp[:, CP_LNG + c:CP_LNG + c + 1], bias=colp[:, CP_LNB + c:CP_LNB + c + 1])
        for s in range(nsub):
            P.tt("pool", h[:, s, :], h[:, s, :], rowp[:, 0, :], ALU.add, [r_h[s], r_const], [r_h[s]])
        proj_tokmajor(nsub, zT, r_zT, 22)
        rmsnorm_to_uT(nsub, CP_G_FFN1)
        ffn(nsub, 24, 32)
        for s in range(nsub):
            P.act(junk[:], h[:, s, :], AF.Square, [r_h[s]], [res("junk"), res("ssq")], accum_out=ssq[:, s:s + 1])
        P.act(rstd[:, 0:nsub], ssq[:, 0:nsub], AF.Ln, [res("ssq")], [res("rstd")], scale=1.0 / D, bias=RMS_EPS)
        P.act(rstd[:, 0:nsub], rstd[:, 0:nsub], AF.Exp, [res("rstd")], [res("rstd")], scale=-0.5)
        ov = ybuf[:].rearrange("p c t -> p (c t)").rearrange("p (s d) -> p s d", s=4)
        outs = []
        for s in range(nsub):
            P.stt(ov[:, s, :], h[:, s, :], rstd[:, s:s + 1], rowp[:, 1, :], ALU.mult, ALU.mult,
                  [r_h[s], res("rstd"), r_const], r_y)
            outs.append(P.dma("pool", outd[tok0 - 128 + s * 128: tok0 - 128 + (s + 1) * 128, :], ov[:, s, :],
                              reads=r_y))
        io.setdefault("_outs", []).extend(outs)


def make_phase_b_nc(tiles=None):
    nc = bass.Bass("TRN2", target_bir_lowering=False)
    if tiles is None:
        tiles = [(0, 1, False)] + [(128 + 512 * i, 4, True) for i in range(8)]
    io = {
        "xB": nc.dram_tensor("xB", [NTB, D], F32, kind="ExternalInput").ap(),
        "oT": nc.dram_tensor("oT", [128, 8, NTB], BF16, kind="ExternalInput").ap(),
        "w32": nc.dram_tensor("w32", [NWT_B, 128, 4096], F32, kind="ExternalInput").ap(),
        "w16": nc.dram_tensor("w16", [NWT_B, 128, 4096], BF16, kind="Internal").ap(),
        "wdg": nc.dram_tensor("wdg", [8, 128, 4096], BF16, kind="Internal").ap(),
        "colp": nc.dram_tensor("colp", [128, NCOLP], F32, kind="ExternalInput").ap(),
        "rowp": nc.dram_tensor("rowp", [128, 2, D], F32, kind="ExternalInput").ap(),
        "ident": nc.dram_tensor("ident", [128, 128], F32, kind="ExternalInput").ap(),
        "out": nc.dram_tensor("out", [NTB - 128, D], F32, kind="ExternalOutput").ap(),
    }
    P = Prog(nc)
    with contextlib.ExitStack() as es:
        C = Ctx(nc, es)
        build_phase_b(nc, P, C, io, tiles)
        fin = P.op("sp", lambda e: e.nop(), (), ())
        fin.deps.extend(io["_outs"])
        P.finish(es)
    return nc, P


WA_COLS = 1792
CA_G = 0
CA_LOG = 8
CA_NG = 14
NCOLA = 16
KA_IDENT, KA_NEGT, KA_NEGU, KA_MASKD, KA_MASKH, KA_ONES, KA_ZERO = range(7)
NCA = 7


def build_phase_a(nc, P, C, io, ntiles, do_attn=True, do_hg=True, out_fn=None, tile_hook=None, out_res=None):
    xA, wA32, colp_d, cst_d, mres_d = (io[k] for k in ("xA", "wA32", "colpA", "constA", "mres"))
    if out_fn is None:
        oTA = io["oTA"]
        out_fn = lambda q, j: oTA[:, q, j * 512:(j + 1) * 512]
    out_w = [] if out_res is None else [out_res]
    TM = 512
    NT = SEQ // TM
    wA = C.sb("wA", [128, 8, WA_COLS], BF16)
    KT = C.sb("KT", [128, 2, SEQ], BF16)
    V = C.sb("V", [128, SEQ // 128, 256], BF16)
    xt = C.sb("xt", [128, 4, D], F32)
    utm = C.sb("utm", [128, 4, D], BF16)
    uT = C.sb("uT", [128, 8, TM], BF16)
    QT = C.sb("QT", [128, 2, TM], BF16)
    hq = C.sb("hq", [128, 2, TM], F32)
    hsgs = C.sb("hsgs", [128, 2, TM], F32)
    hgt = C.sb("hgt", [128, 2, TM], F32)
    hi = C.sb("hi", [128, 4, 256], BF16)
    junk = C.sb("junk", [128, D], BF16)
    ssq = C.sb("ssq", [128, 4], F32)
    rstd = C.sb("rstd", [128, 4], F32)
    colp = C.sb("colpA", [128, NCOLA], F32)
    cst = C.sb("cst", [128, NCA, 128], BF16)
    mres = C.sb("mres", [128, TM], F32)
    lbt = C.sb("lbt", [128, 2, 4], F32)
    lb = C.sb("lb", [128, 2], F32)
    oml = C.sb("oml", [128, 2], F32)
    eZ = [C.sb(f"eZ{i}", [128, 2, TM], F32) for i in range(3)]
    spb = [C.sb(f"spb{i}", [128, 2, TM], BF16) for i in range(2)]
    Xb = [C.sb(f"Xb{i}", [128, 2, TM], BF16) for i in range(2)]
    Wb = [C.sb(f"Wb{i}", [128, 2, TM], BF16) for i in range(2)]
    osb = [C.sb(f"osb{i}", [128, TM], BF16) for i in range(2)]
    hff = C.sb("hff", [128, TM], F32)
    hg_ = C.sb("hg_", [128, TM], F32)
    hkk = C.sb("hkk", [128, TM], F32)
    hG = C.sb("hG", [128, TM], F32)
    heG = C.sb("heG", [128, TM], F32)
    hks = C.sb("hks", [128, TM], BF16)
    hqd = C.sb("hqd", [128, TM], BF16)
    hkib = C.sb("hkib", [128, TM], BF16)
    kstm = C.sb("kstm", [128, 4, 128], BF16)
    smk = [C.sb(f"smk{i}", [128, 128], BF16) for i in range(2)]
    Sst = [[C.sb(f"S{hd}_{i}", [128, 128], F32) for i in range(2)] for hd in range(2)]
    Sbf = [[C.sb(f"Sb{hd}_{i}", [128, 128], BF16) for i in range(2)] for hd in range(2)]
    osq = C.sb("osq", [128, TM], BF16)
    hrs = C.sb("hrs", [128, TM], F32)
    gsig = C.sb("gsig", [128, 2, TM], F32)
    ho1 = C.sb("ho1", [128, TM], F32)
    ohT = [C.sb(f"ohT{i}", [128, TM], BF16) for i in range(2)]
    pz0 = C.es.enter_context(nc.psum_tensor("p_z0", [128, 2, 512], F32))
    pz1 = C.es.enter_context(nc.psum_tensor("p_z1", [128, 2, 512], F32))
    pc = C.es.enter_context(nc.psum_tensor("p_c", [128, 2, 512], F32))
    psing = [C.ps(f"pa{i}") for i in range(6, 8)]
    pzz = (pz0, pz1)

    def pbk(b):
        if b < 2:
            return pz0[:, b, :]
        if b < 4:
            return pz1[:, b - 2, :]
        if b < 6:
            return pc[:, b - 4, :]
        return psing[b - 6][:]

    R = {}

    def res(n):
        if n not in R:
            R[n] = Res(n)
        return R[n]

    r_pb = [R.setdefault(f"pb{i}", Res(f"pb{i}", True)) for i in range(8)]
    r_const = res("const")
    r_xt = [res(f"xt{s}") for s in range(4)]
    r_utm = [res(f"utm{s}") for s in range(4)]
    r_uT = [res(f"uT{c}") for c in range(8)]
    r_KT = [[res(f"KT{p}_{j}") for j in range(NT)] for p in range(2)]
    r_V = [res(f"V{j}") for j in range(NT)]

    ident = cst[:, KA_IDENT, :]
    negT = cst[:, KA_NEGT, :]
    negU = cst[:, KA_NEGU, :]
    maskD = cst[:, KA_MASKD, :]
    maskH = cst[:, KA_MASKH, :]
    ones128 = cst[:, KA_ONES, :]
    zeroM = cst[:, KA_ZERO, :]

    P.dma("sp", colp[:], colp_d[:, :], writes=[r_const])
    P.dma("sp", mres[:], mres_d[:, :], writes=[r_const])
    P.dma("pool", cst[:], cst_d[:, :, :], writes=[r_const])
    for k in range(8):
        P.dma("pool", wA[:, k, :], wA32[:, k, :], writes=[res("wA")])
    lg = colp[:, CA_LOG:CA_LOG + 6].rearrange("p (h r) -> p h r", h=2)
    P.act(lbt[:, :, 0:3], lg, AF.Exp, [r_const], [res("lbt")])
    for hd in range(2):
        P.tt("dve", lbt[:, hd, 3:4], lbt[:, hd, 0:1], lbt[:, hd, 1:2], ALU.add, [res("lbt")], [res("lbt")])
        P.tt("dve", lbt[:, hd, 3:4], lbt[:, hd, 3:4], lbt[:, hd, 2:3], ALU.add, [res("lbt")], [res("lbt")])
        P.op("dve", (lambda hd: lambda e: e.reciprocal(out=lbt[:, hd, 3:4], in_=lbt[:, hd, 3:4]))(hd),
             [res("lbt")], [res("lbt")])
        P.tt("dve", lb[:, hd:hd + 1], lbt[:, hd, 0:1], lbt[:, hd, 3:4], ALU.mult, [res("lbt")], [res("lb")])
    P.ts("dve", oml[:], lb[:], -1.0, 1.0, ALU.mult, ALU.add, [res("lb")], [res("lb2")])
    for hd in range(2):
        P.memset("pool", Sst[hd][0][:], 0.0, [res(f"S{hd}_0")])
        P.memset("pool", Sbf[hd][0][:], 0.0, [res(f"Sb{hd}_0")])
    hstep = [0, 0]

    outs = io.setdefault("_outs", [])
    io["_zeroM"] = zeroM
    r_p7 = [r_pb[7], r_pb[7]]
    pending_norm = []

    def norm_thunks(jn):
        th = []

        def n0():
            for s in range(4):
                P.dma("sp", xt[:, s, :], xA[jn * TM + s * 128: jn * TM + (s + 1) * 128, :], writes=[r_xt[s]])

        def n1():
            for s in range(4):
                P.act(junk[:], xt[:, s, :], AF.Square, [r_xt[s]], [res("junk"), res("ssq")],
                      accum_out=ssq[:, s:s + 1])

        def n2():
            P.act(rstd[:], ssq[:], AF.Ln, [res("ssq")], [res("rstd")], scale=1.0 / D, bias=RMS_EPS)
            P.act(rstd[:], rstd[:], AF.Exp, [res("rstd")], [res("rstd")], scale=-0.5)

        def n3():
            for s in range(4):
                P.ts("dve", utm[:, s, :], xt[:, s, :], rstd[:, s:s + 1], None, ALU.mult, None,
                     [r_xt[s], res("rstd")], [r_utm[s]])
        th.extend([n0, n1, n2, n3])
        pT = pbk(7).bitcast(BF16)

        def ntr(c):
            h_ = c % 2
            for s in range(4):
                P.tr(pT[:, h_ * 512 + s * 128:h_ * 512 + (s + 1) * 128], utm[:, s, c * 128:(c + 1) * 128], ident,
                     [r_utm[s], r_const], [r_p7[h_]])

        def nev(c):
            h_ = c % 2
            P.ts("dve", uT[:, c, :], pT[:, h_ * 512:h_ * 512 + TM], colp[:, CA_G + c:CA_G + c + 1], None, ALU.mult,
                 None, [r_p7[h_], r_const], [r_uT[c]])
        for c in range(8):
            th.append((lambda c: lambda: ntr(c))(c))
            th.append((lambda c: lambda: nev(c))(c))
        return th

    for j in range(ntiles):
        if tile_hook is not None:
            tile_hook(j)
        if j == 0:
            for f_ in norm_thunks(0):
                f_()
        else:
            while pending_norm:
                pending_norm.pop(0)()
        for fc in range(10):
            bank = 4 + (fc % 2)
            for k in range(8):
                P.mm(pbk(bank), wA[:, k, fc * 128:(fc + 1) * 128], uT[:, k, :], k == 0, k == 7,
                     [res("wA"), r_uT[k]], [r_pb[bank]])
            if fc < 2:
                P.ts("dve", QT[:, fc, :], pbk(bank), 0.125, None, ALU.mult, None, [r_pb[bank]], [res(f"QT{fc}")])
            elif fc < 4:
                P.copy("dve", KT[:, fc - 2, j * TM:(j + 1) * TM], pbk(bank), [r_pb[bank]], [r_KT[fc - 2][j]])
            elif fc < 6:
                P.copy("dve", hq[:, fc - 4, :], pbk(bank), [r_pb[bank]], [res(f"hq{fc - 4}")])
            elif fc < 8:
                P.act(hsgs[:, fc - 6, :], pbk(bank), AF.Sigmoid, [r_pb[bank]], [res(f"hsg{fc - 6}")])
            else:
                P.copy("dve", hgt[:, fc - 8, :], pbk(bank), [r_pb[bank]], [res(f"hgt{fc - 8}")])
                P.act(gsig[:, fc - 8, :], pbk(bank), AF.Sigmoid, [r_pb[bank]], [res(f"gsig{fc - 8}")])
        for s in range(4):
            bank = 4 + s % 2
            for k in range(8):
                P.mm(pbk(bank), uT[:, k, s * 128:(s + 1) * 128], wA[:, k, 1280:1792], k == 0, k == 7,
                     [res("wA"), r_uT[k]], [r_pb[bank]])
            P.copy("dve", V[:, 4 * j + s, :], pbk(bank)[:, 0:256], [r_pb[bank]], [r_V[j]])
            P.act(hi[:, s, :], pbk(bank)[:, 256:512], AF.Copy, [r_pb[bank]], [res(f"hi{s}")])

        def hgrn_thunks(hd):
            rt = lambda n: res(f"hg_{n}")
            hsg = hsgs[:, hd, :]
            r_sg = res(f"hsg{hd}")
            gsl = gsig[:, hd, :]
            r_gs = res(f"gsig{hd}")
            OB = 3
            SB_ = 2
            th = []

            def p1a():
                P.ts("dve", hff[:], hsg, oml[:, hd:hd + 1], lb[:, hd:hd + 1], ALU.mult, ALU.add,
                     [r_sg, res("lb"), res("lb2")], [rt("f")])
                P.stt(gsl, hgt[:, hd, :], colp[:, CA_NG + hd:CA_NG + hd + 1], gsl, ALU.mult, ALU.mult,
                      [res(f"hgt{hd}"), r_gs, r_const], [r_gs])

            def p1b():
                P.act(hg_[:], hff[:], AF.Ln, [rt("f")], [rt("g")])
                P.ts("pool", hkk[:], hff[:], -1.0, 1.0, ALU.mult, ALU.add, [rt("f")], [rt("kk")])

            def p1c():
                P.op("dve", lambda e: e.tensor_tensor_scan(out=hG[:], data0=mres[:], data1=hg_[:], initial=0.0,
                                                           op0=ALU.mult, op1=ALU.add),
                     [rt("g"), r_const], [rt("G")])

            def p2a():
                P.act(heG[:], hG[:], AF.Exp, [rt("G")], [rt("eG")])
                P.act(hsg, hG[:], AF.Exp, [rt("G"), rt("f")], [r_sg], scale=-1.0)

            def p2b():
                P.tt("dve", hqd[:], hq[:, hd, :], heG[:], ALU.mult, [res(f"hq{hd}"), rt("eG")], [rt("qd")])
                P.tt("pool", hg_[:], hkk[:], hsg, ALU.mult, [rt("kk"), r_sg, rt("G")], [rt("g")])

            def p2c():
                P.copy("pool", hkib[:], hg_[:], [rt("g")], [rt("kib")])
                for ch in range(16):
                    P.ts("pool", hks[:, ch * 32:(ch + 1) * 32], hg_[:, ch * 32:(ch + 1) * 32],
                         heG[:, ch * 32 + 31:ch * 32 + 32], None, ALU.mult, None, [rt("g"), rt("eG")], [rt("ks")])

            def p3a():
                pT = pbk(7).bitcast(BF16)
                for s in range(4):
                    P.tr(pT[:, s * 128:(s + 1) * 128], hks[:, s * 128:(s + 1) * 128], ident, [rt("ks"), r_const],
                         [r_p7[0]])

            def p3b():
                pT = pbk(7).bitcast(BF16)
                P.copy("dve", kstm[:].rearrange("p s d -> p (s d)"), pT[:, 0:TM], [r_p7[0]], [rt("kstm")])
            th.extend([p1a, p1b, p1c, p2a, p2b, p2c, p3a, p3b])

            def bh_a(s):
                tsl = slice(s * 128, (s + 1) * 128)
                P.mm(pbk(SB_)[:, 0:128], hkib[:, tsl], hqd[:, tsl], True, True, [rt("kib"), rt("qd")], [r_pb[SB_]])

            def bh_b(s):
                P.tt("dve", smk[s % 2][:], pbk(SB_)[:, 0:128], maskH, ALU.mult, [r_pb[SB_], r_const],
                     [rt(f"smk{s % 2}")])

            def bh_c(s):
                tsl = slice(s * 128, (s + 1) * 128)
                P.mm(pbk(OB)[:, tsl], hi[:, s, hd * 128:(hd + 1) * 128], smk[s % 2][:], True, False,
                     [res(f"hi{s}"), rt(f"smk{s % 2}")], [r_pb[OB]])

            def cs_a(s, cc):
                n = hstep[hd]
                cur = n % 2
                csl = slice(s * 128 + cc * 32, s * 128 + (cc + 1) * 32)
                P.mm(pbk(OB)[:, csl], Sbf[hd][cur][:], hqd[:, csl], False, cc == 3,
                     [res(f"Sb{hd}_{cur}"), rt("qd")], [r_pb[OB]])
                psl = slice(cc * 32, (cc + 1) * 32)
                P.op("pe", lambda e: e.matmul(
                    pbk(SB_)[:, 128:256], lhsT=kstm[psl, s, :], rhs=hi[psl, s, hd * 128:(hd + 1) * 128],
                    start=True, stop=True, tile_position=(cc * 32, 0)),
                     [rt("kstm"), res(f"hi{s}")], [r_pb[SB_]])

            def cs_b(s, cc):
                n = hstep[hd]
                cur, nxt = n % 2, (n + 1) % 2
                ch = s * 4 + cc
                P.stt(Sst[hd][nxt][:], Sst[hd][cur][:], heG[:, ch * 32 + 31:ch * 32 + 32], pbk(SB_)[:, 128:256],
                      ALU.mult, ALU.add, [res(f"S{hd}_{cur}"), rt("eG"), r_pb[SB_]], [res(f"S{hd}_{nxt}")])
                P.copy("pool", Sbf[hd][nxt][:], Sst[hd][nxt][:], [res(f"S{hd}_{nxt}")], [res(f"Sb{hd}_{nxt}")])
                hstep[hd] += 1

            for s in range(4):
                th.append((lambda s: lambda: bh_a(s))(s))
                th.append((lambda s: lambda: bh_b(s))(s))
                th.append((lambda s: lambda: bh_c(s))(s))
                for cc in range(4):
                    th.append((lambda s, cc: lambda: cs_a(s, cc))(s, cc))
                    th.append((lambda s, cc: lambda: cs_b(s, cc))(s, cc))

            def f1a():
                P.act(osq[:], pbk(OB), AF.Square, [r_pb[OB]], [rt("osq")])

            def f1b():
                P.mm(pbk(SB_), ones128, osq[:], True, True, [r_const, rt("osq")], [r_pb[SB_]])

            def f1c():
                P.act(hrs[:], pbk(SB_), AF.Ln, [r_pb[SB_]], [rt("rs")], bias=RMS_EPS)
                P.act(hrs[:], hrs[:], AF.Exp, [rt("rs")], [rt("rs")], scale=-0.5)

            def f2a():
                P.tt("dve", ho1[:], pbk(OB), hrs[:], ALU.mult, [r_pb[OB], rt("rs")], [rt("o1")])

            def f2b():
                P.tt("pool", ohT[hd][:], ho1[:], gsl, ALU.mult, [rt("o1"), r_gs], [res(f"ohT{hd}")])
                outs.append(P.dma("pool", out_fn(2 + hd, j), ohT[hd][:], reads=[res(f"ohT{hd}")], writes=out_w))
            th.extend([f1a, f1b, f1c, f2a, f2b])
            return th

        if do_hg and not do_attn:
            for hd in range(2):
                for f_ in hgrn_thunks(hd):
                    f_()

        if do_attn:
            for p in range(2):
                cb = (4, 5)
                ob = 6
                for hh in range(2):
                    P.mm(pbk(cb[hh]), zeroM, wA[:, 0, 0:512], True, False, [r_const, res("wA")], [r_pb[cb[hh]]])
                P.mm(pbk(ob), zeroM, wA[:, 0, 0:512], True, False, [r_const, res("wA")], [r_pb[ob]])
                nkb = 4 * j + 4
                kbs = list(range(nkb - 1, -1, -1))
                extra = hgrn_thunks(p) if do_hg else []
                if p == 1 and j + 1 < ntiles:
                    nt = norm_thunks(j + 1)
                    mixed = []
                    while extra or nt:
                        if extra:
                            mixed.append(extra.pop(0))
                        if extra:
                            mixed.append(extra.pop(0))
                        if extra:
                            mixed.append(extra.pop(0))
                        if nt:
                            mixed.append(nt.pop(0))
                    extra = mixed
                per_it = (len(extra) + nkb - 1) // nkb

                def q0_of(kb):
                    m = kb - 4 * j
                    return 128 * m if m > 0 else 0

                def st_qk(t):
                    kb = kbs[t]
                    q0 = q0_of(kb)
                    zi = 0
                    ksl = slice(kb * 128, (kb + 1) * 128)
                    for hh in range(2):
                        prt = slice(hh * 64, (hh + 1) * 64)
                        P.mm(pbk(2 * zi + hh)[:, q0:TM], KT[prt, p, ksl], QT[prt, p, q0:TM], True, True,
                             [r_KT[p][kb // 4], res(f"QT{p}")], [r_pb[2 * zi + hh]])

                def st_act(t):
                    kb = kbs[t]
                    q0 = q0_of(kb)
                    zi, e3, i2 = 0, t % 3, t % 2
                    P.act(eZ[e3][:, :, q0:TM], pzz[zi][:, :, q0:TM], AF.Exp, [r_pb[2 * zi], r_pb[2 * zi + 1]],
                          [res(f"eZ{e3}")])
                    if kb - 4 * j >= 0:
                        for hh in range(2):
                            P.tt("pool", eZ[e3][:, hh, q0:q0 + 128], eZ[e3][:, hh, q0:q0 + 128], maskD, ALU.mult,
                                 [res(f"eZ{e3}"), r_const], [res(f"eZ{e3}")])
                    P.act(spb[i2][:, :, q0:TM], eZ[e3][:, :, q0:TM], AF.Ln, [res(f"eZ{e3}")], [res(f"sp{i2}")],
                          bias=1.0)

                def st_negT(t):
                    q0 = q0_of(kbs[t])
                    i2 = t % 2
                    for hh in range(2):
                        P.mm(pbk(cb[hh])[:, q0:TM], negT, spb[i2][:, hh, q0:TM], False, False,
                             [r_const, res(f"sp{i2}")], [r_pb[cb[hh]]])

                def st_expc(t):
                    q0 = q0_of(kbs[t])
                    i2, e3 = t % 2, t % 3
                    P.act(Xb[i2][:, :, q0:TM], pc[:, :, q0:TM], AF.Exp, [r_pb[4], r_pb[5]], [res(f"X{i2}")])
                    P.tt("dve", Wb[i2][:, :, q0:TM], eZ[e3][:, :, q0:TM], Xb[i2][:, :, q0:TM], ALU.mult,
                         [res(f"eZ{e3}"), res(f"X{i2}")], [res(f"W{i2}")])

                def st_negU(t):
                    q0 = q0_of(kbs[t])
                    i2 = t % 2
                    for hh in range(2):
                        P.mm(pbk(cb[hh])[:, q0:TM], negU, spb[i2][:, hh, q0:TM], False, False,
                             [r_const, res(f"sp{i2}")], [r_pb[cb[hh]]])

                def st_pv(t):
                    kb = kbs[t]
                    q0 = q0_of(kb)
                    i2 = t % 2
                    for hh in range(2):
                        P.mm(pbk(ob)[hh * 64:(hh + 1) * 64, q0:TM],
                             V[:, kb, p * 128 + hh * 64:p * 128 + (hh + 1) * 64],
                             Wb[i2][:, hh, q0:TM], False, kb == 0, [r_V[kb // 4], res(f"W{i2}")], [r_pb[ob]])

                n = nkb
                st_qk(0)
                st_act(0)
                if n > 1:
                    st_qk(1)
                    st_act(1)
                st_negT(0)
                for t in range(n):
                    if t + 2 < n:
                        st_qk(t + 2)
                    st_expc(t)
                    if t + 1 < n:
                        st_negU(t)
                        st_negT(t + 1)
                    if t >= 1:
                        st_pv(t - 1)
                    if t + 2 < n:
                        st_act(t + 2)
                    for _ in range(per_it):
                        if extra:
                            extra.pop(0)()
                st_pv(n - 1)
                while extra:
                    extra.pop(0)()
                P.act(osb[p][:], pbk(ob), AF.Copy, [r_pb[ob]], [res(f"osb{p}")])
                outs.append(P.dma("pool", out_fn(p, j), osb[p][:], reads=[res(f"osb{p}")], writes=out_w))


def make_phase_a_nc(ntiles=16, do_attn=True, do_hg=True):
    nc = bass.Bass("TRN2", target_bir_lowering=False)
    io = {
        "xA": nc.dram_tensor("xA", [SEQ, D], F32, kind="ExternalInput").ap(),
        "wA32": nc.dram_tensor("wA32", [128, 8, WA_COLS], F32, kind="ExternalInput").ap(),
        "colpA": nc.dram_tensor("colpA", [128, NCOLA], F32, kind="ExternalInput").ap(),
        "constA": nc.dram_tensor("constA", [128, NCA, 128], F32, kind="ExternalInput").ap(),
        "mres": nc.dram_tensor("mres", [128, 512], F32, kind="ExternalInput").ap(),
        "oTA": nc.dram_tensor("oTA", [128, 4, SEQ], BF16, kind="ExternalOutput").ap(),
    }
    P = Prog(nc)
    with contextlib.ExitStack() as es:
        C = Ctx(nc, es)
        build_phase_a(nc, P, C, io, ntiles, do_attn, do_hg)
        fin = P.op("sp", lambda e: e.nop(), (), ())
        fin.deps.extend(io["_outs"])
        P.finish(es)
    return nc, P, C


def phase_a_inputs(inp, b, g):
    w = inp["w_in_ab"][0]
    cols = []
    for base in (0, 512):
        cols.append(w[:, base + 256 * g: base + 256 * g + 256])
    for base in (1536, 2048, 3072):
        cols.append(w[:, base + 256 * g: base + 256 * g + 256])
    cols.append(w[:, 1024 + 256 * g: 1024 + 256 * g + 256])
    cols.append(w[:, 2560 + 256 * g: 2560 + 256 * g + 256])
    wsel = np.concatenate(cols, axis=1)
    wA32 = np.ascontiguousarray(wsel.reshape(8, 128, WA_COLS).transpose(1, 0, 2))
    colp = np.zeros((128, NCOLA), np.float32)
    colp[:, CA_G:CA_G + 8] = colvec(inp["norm_mix_g"][0])
    lg = inp["hg_lb_logits"]
    for hd in range(2):
        head = 2 * g + hd
        colp[:, CA_LOG + hd * 3: CA_LOG + hd * 3 + 3] = lg[:, head * 128:(head + 1) * 128].T
        colp[:, CA_NG + hd] = inp["hg_norm_g"][0][head]
    return {"xA": np.ascontiguousarray(inp["x"][b]), "wA32": wA32, "colpA": colp}


def phase_a_consts():
    cst = np.zeros((128, NCA, 128), np.float32)
    jj = np.arange(128)[:, None]
    ss = np.arange(128)[None, :]
    cst[:, KA_IDENT, :] = np.eye(128)
    cst[:, KA_NEGT, :] = -1.0 * (jj >= ss)
    cst[:, KA_NEGU, :] = -1.0 * (jj < ss)
    cst[:, KA_MASKD, :] = (jj < ss)
    cst[:, KA_MASKH, :] = (jj <= ss) & ((jj // 32) == (ss // 32))
    cst[:, KA_ONES, :] = 1.0 / 128.0
    mres = np.ones((128, 512), np.float32)
    mres[:, ::32] = 0.0
    return cst, mres


GCOLS = 128 + SEQ
PAIRS = [[0, 1], [2, 3], [4, 5], [6, 7]]


def make_fused_nc(ntiles_a=16, tiles_b=None):
    nc = bass.Bass("TRN2", target_bir_lowering=False)
    if tiles_b is None:
        tiles_b = [(0, 1, False)] + [(128 + 512 * i, 4, True) for i in range(8)]
    io = {
        "xA": nc.dram_tensor("xA", [SEQ, D], F32, kind="ExternalInput").ap(),
        "wA32": nc.dram_tensor("wA32", [128, 8, WA_COLS], F32, kind="ExternalInput").ap(),
        "colpA": nc.dram_tensor("colpA", [128, NCOLA], F32, kind="ExternalInput").ap(),
        "constA": nc.dram_tensor("constA", [128, NCA, 128], F32, kind="ExternalInput").ap(),
        "mres": nc.dram_tensor("mres", [128, 512], F32, kind="ExternalInput").ap(),
        "xB": nc.dram_tensor("xB", [NTB, D], F32, kind="ExternalInput").ap(),
        "w32": nc.dram_tensor("w32", [NWT_B, 128, 4096], F32, kind="ExternalInput").ap(),
        "w16": nc.dram_tensor("w16", [NWT_B, 128, 4096], BF16).ap(),
        "wdg": nc.dram_tensor("wdg", [8, 128, 4096], BF16).ap(),
        "colp": nc.dram_tensor("colp", [128, NCOLP], F32, kind="ExternalInput").ap(),
        "rowp": nc.dram_tensor("rowp", [128, 2, D], F32, kind="ExternalInput").ap(),
        "ident": nc.dram_tensor("ident", [128, 128], F32, kind="ExternalInput").ap(),
        "out": nc.dram_tensor("out", [NTB - 128, D], F32, kind="ExternalOutput").ap(),
    }
    cc_src = [nc.dram_tensor(f"cc_src{j}", [512, 512], BF16) for j in range(16)]
    cc_dst = [nc.dram_tensor(f"cc_dst{j}", [1024, 512], BF16) for j in range(16)]
    cc_pad = nc.dram_tensor("cc_pad", [1024, 128], BF16)
    P = Prog(nc)
    with contextlib.ExitStack() as es0:
        P.alloc_sems(es0)
        r_dst = Res("cc_dst")
        pre = precast_ops(io)
        pre_res = {}
        per_tile = (len(pre) + ntiles_a - 1) // ntiles_a
        state = {"ccops": []}

        def gather_tile(j):
            deps = io["_outs"][-4:]
            op = P.cc((lambda j: lambda e: e.collective_compute(
                "AllGather", ALU.bypass, replica_groups=PAIRS,
                ins=[cc_src[j].ap().opt()], outs=[cc_dst[j].ap().opt()]))(j), [], [])
            op.deps.extend(deps)
            state["ccops"].append(op)

        def hook(j):
            if j > 0:
                gather_tile(j - 1)
            for a in pre[j * per_tile:(j + 1) * per_tile]:
                r = pre_res.setdefault(a[2], Res(a[2]))
                P.dma("pool", a[0], a[1], writes=[r])

        with contextlib.ExitStack() as esA:
            C = Ctx(nc, esA)
            build_phase_a(nc, P, C, io, ntiles_a,
                          out_fn=lambda q, j: cc_src[j].ap()[q * 128:(q + 1) * 128, :],
                          tile_hook=hook)
            gather_tile(ntiles_a - 1)
            zpad = [P.dma("pool", cc_pad.ap()[q * 128:(q + 1) * 128, :], io["_zeroM"]) for q in range(8)]
            io["_outs"] = []
            fin_cc = P.op("pool", lambda e: e.nop(), (), [r_dst])
            fin_cc.deps.extend(state["ccops"][-1:] + zpad)
            P.barrier()
            P.emit_segment()
            sbA = C.sb_bytes
        with contextlib.ExitStack() as esB:
            C = Ctx(nc, esB, "b_")

            def gsrc(half, tok0, T):
                t = 4096 * half - 128 + tok0
                if t < 0:
                    return cc_pad.ap().rearrange("(k p) t -> p k t", p=128)[:, :, 0:T]
                j, c0 = divmod(t, 512)
                return cc_dst[j].ap().rearrange("(k p) t -> p k t", p=128)[:, :, c0:c0 + T]

            build_phase_b(nc, P, C, io, tiles_b, precast_done=pre_res, gathered=gsrc, gathered_res=r_dst)
            fin = P.op("sp", lambda e: e.nop(), (), ())
            fin.deps.extend(io["_outs"])
            P.emit_segment()
            sbB = C.sb_bytes
    P.sb_bytes = (sbA, sbB)
    return nc, P


def wtile_kn(W, col0, ncols=512):
    return np.ascontiguousarray(W[:, col0:col0 + ncols].reshape(8, 128, ncols).transpose(1, 0, 2)).reshape(128, -1)


def phase_b_weights(inp, w_out_perm):
    tiles = []
    for half in range(2):
        tiles.append(wtile_kn(w_out_perm, half * 512))
    for layer in range(2):
        pass
    w1 = inp["w_ff1"]
    w2 = inp["w_ff2"]
    wg = inp["conv_w_glu"][0]
    wp = inp["conv_w_pw"][0]

    def ff1_tiles(l):
        return [wtile_kn(w1[l], j * 512) for j in range(8)]

    def ff2_tiles(l):
        out = []
        for half in range(2):
            for cg in range(4):
                blk = w2[l][cg * 1024:(cg + 1) * 1024, half * 512:(half + 1) * 512]
                out.append(np.ascontiguousarray(blk.reshape(8, 128, 512).transpose(1, 0, 2)).reshape(128, -1))
        return out

    tiles += ff1_tiles(0) + ff2_tiles(0)
    for j in range(4):
        cols = []
        for q in range(2):
            cols.append(wg[:, (2 * j + q) * 128:(2 * j + q + 1) * 128])
        for q in range(2):
            cols.append(wg[:, 1024 + (2 * j + q) * 128:1024 + (2 * j + q + 1) * 128])
        blk = np.concatenate(cols, axis=1)
        tiles.append(wtile_kn(blk, 0))
    for half in range(2):
        tiles.append(wtile_kn(wp, half * 512))
    tiles += ff1_tiles(1) + ff2_tiles(1)
    assert len(tiles) == NWT_B
    return np.stack(tiles).astype(np.float32)


def colvec(v):
    return np.ascontiguousarray(v.reshape(8, 128).T)


def phase_b_params(inp, halo_flag):
    colp = np.zeros((128, NCOLP), np.float32)
    colp[:, CP_G_FFN0:CP_G_FFN0 + 8] = colvec(inp["norm_ffn_g"][0])
    colp[:, CP_G_MIX1:CP_G_MIX1 + 8] = colvec(inp["norm_mix_g"][1])
    colp[:, CP_G_FFN1:CP_G_FFN1 + 8] = colvec(inp["norm_ffn_g"][1])
    colp[:, CP_BA:CP_BA + 8] = colvec(inp["conv_b_glu"][0][:1024])
    colp[:, CP_BB:CP_BB + 8] = colvec(inp["conv_b_glu"][0][1024:])
    colp[:, CP_BDW:CP_BDW + 8] = colvec(inp["conv_b_dw"][0])
    colp[:, CP_LNG:CP_LNG + 8] = colvec(inp["conv_ln_g"][0])
    colp[:, CP_LNB:CP_LNB + 8] = colvec(inp["conv_ln_b"][0])
    colp[:, CP_HALO] = halo_flag
    colp[:, CP_F0] = 1.0 - halo_flag
    colp[:, CP_F1] = halo_flag
    wdw = inp["conv_w_dw"][0]
    for c in range(8):
        colp[:, CP_WDW + c * CONVW: CP_WDW + (c + 1) * CONVW] = wdw[:, c * 128:(c + 1) * 128].T
    rowp = np.zeros((128, 2, D), np.float32)
    rowp[:, 0, :] = inp["conv_b_pw"][0][None, :]
    rowp[:, 1, :] = inp["final_norm_g"][None, :]
    return colp, rowp


def w_out_permuted(inp):
    w = inp["w_out_ab"][0]
    rows = []
    for gg in range(2):
        rows.append(w[256 * gg: 256 * gg + 256])
        rows.append(w[512 + 256 * gg: 512 + 256 * gg + 256])
    return np.concatenate(rows, axis=0)


def kernel(**inputs):
    inp = {k: np.asarray(v) for k, v in inputs.items()}
    cores = list(range(NCORES))
    nc, _ = make_fused_nc()
    cst, mres = phase_a_consts()
    w32 = phase_b_weights(inp, w_out_permuted(inp))
    ident = np.eye(128, dtype=np.float32)
    maps = []
    for c in cores:
        b, g = divmod(c, 2)
        m = phase_a_inputs(inp, b, g)
        m["constA"] = cst
        m["mres"] = mres
        colp, rowp = phase_b_params(inp, float(g))
        xB = np.zeros((NTB, D), np.float32)
        t0 = 4096 * g - 128
        lo = max(t0, 0)
        xB[lo - t0:] = inp["x"][b, lo:t0 + NTB]
        m.update({"xB": xB, "w32": w32, "colp": colp, "rowp": rowp, "ident": ident})
        maps.append(m)
    res = run_bass_kernel_spmd(nc, maps, core_ids=cores)
    out = np.zeros((BATCH, SEQ, D), np.float32)
    for c in cores:
        b, g = divmod(c, 2)
        out[b, 4096 * g:4096 * (g + 1)] = np.asarray(res.results[c]["out"])
    return out
```
